# Optimizing a Trainium2 kernel written in Bass

```python
import jax, jax.numpy as jnp
from jax import lax
import numpy as np

D_MODEL = 2048
BATCH = 1
SEQ = 8192
DEPTH = 1
DEC_BATCH = 8
DEC_SEQ = 2048
PAST_LEN = 128

GRID_W = 64
HEAD_DIM = 128
NA_HEADS = 8
NA_WIDTH = NA_HEADS * HEAD_DIM
NA_KH_MAX = 8
NA_KW = 16
GQA_HEADS = 8
GQA_KV_HEADS = 2
GQA_WIDTH = GQA_HEADS * HEAD_DIM
GQA_KV_WIDTH = GQA_KV_HEADS * HEAD_DIM
ROPE_THETA = 10000.0
ROPE_AXIS_DIM = HEAD_DIM // 2
Q_BLOCK = 128
EPS = 1e-6
IN_SIZES = (NA_WIDTH, NA_WIDTH, NA_WIDTH, NA_WIDTH,
            GQA_WIDTH, GQA_KV_WIDTH, GQA_KV_WIDTH, GQA_WIDTH,
            D_MODEL, D_MODEL)
IN_WIDTH = 4 * NA_WIDTH + 2 * GQA_WIDTH + 2 * GQA_KV_WIDTH + 2 * D_MODEL

kernel_name = "hybrid_natten_gqa_gated_encoder"


def _split_points():
    pts = []
    acc = 0
    for s in IN_SIZES[:-1]:
        acc += s
        pts.append(acc)
    return pts


def rmsnorm(x, g):
    xf = x.astype(jnp.float32)
    y = xf * lax.rsqrt(jnp.mean(xf * xf, axis=-1, keepdims=True) + EPS)
    return (y * g.astype(jnp.float32)).astype(x.dtype)


def neighbourhood_attention(q, k, v, rpb):
    B, L, H, Dh = q.shape
    rows = L // GRID_W
    kh = min(NA_KH_MAX, rows)
    qg = q.reshape(B, rows, GRID_W, H, Dh)
    kg = k.reshape(B, rows, GRID_W, H, Dh)
    vg = v.reshape(B, rows, GRID_W, H, Dh)
    col = jnp.arange(GRID_W)
    col_start = jnp.clip(col - NA_KW // 2, 0, GRID_W - NA_KW)
    col_idx = col_start[:, None] + jnp.arange(NA_KW)[None, :]
    dc_idx = col_idx - col[:, None] + (NA_KW - 1)
    scale = Dh ** -0.5

    def one_row(r):
        rs = jnp.clip(r - kh // 2, 0, rows - kh)
        q_r = lax.dynamic_index_in_dim(qg, r, axis=1, keepdims=False)
        k_r = lax.dynamic_slice_in_dim(kg, rs, kh, axis=1)
        v_r = lax.dynamic_slice_in_dim(vg, rs, kh, axis=1)
        k_n = k_r[:, :, col_idx]
        v_n = v_r[:, :, col_idx]
        s = jnp.einsum('bqhd,bjqkhd->bhqjk', q_r, k_n,
                       preferred_element_type=jnp.float32) * scale
        dr_idx = rs + jnp.arange(kh) - r + (NA_KH_MAX - 1)
        bias = rpb[:, dr_idx][:, :, dc_idx]
        s = s + bias.transpose(0, 2, 1, 3).astype(jnp.float32)[None]
        p = jax.nn.softmax(s.reshape(B, H, GRID_W, kh * NA_KW), axis=-1)
        p = p.reshape(B, H, GRID_W, kh, NA_KW).astype(v.dtype)
        return jnp.einsum('bhqjk,bjqkhd->bqhd', p, v_n)

    out = lax.map(one_row, jnp.arange(rows))
    return out.transpose(1, 0, 2, 3, 4).reshape(B, L, H * Dh)


def axial_rope_tables(L):
    t = jnp.arange(L)
    r = (t // GRID_W).astype(jnp.float32)
    c = (t % GRID_W).astype(jnp.float32)
    freqs = ROPE_THETA ** (-jnp.arange(0, ROPE_AXIS_DIM, 2, dtype=jnp.float32) / ROPE_AXIS_DIM)
    ang_r = r[:, None] * freqs[None, :]
    ang_c = c[:, None] * freqs[None, :]
    return jnp.cos(ang_r), jnp.sin(ang_r), jnp.cos(ang_c), jnp.sin(ang_c)


def _rotate(x, cos, sin):
    half = x.shape[-1] // 2
    x1, x2 = x[..., :half], x[..., half:]
    c = cos[None, :, None, :]
    s = sin[None, :, None, :]
    return jnp.concatenate([x1 * c - x2 * s, x2 * c + x1 * s], axis=-1)


def apply_axial_rope(x, tables):
    cos_r, sin_r, cos_c, sin_c = tables
    xf = x.astype(jnp.float32)
    y = jnp.concatenate([_rotate(xf[..., :ROPE_AXIS_DIM], cos_r, sin_r),
                         _rotate(xf[..., ROPE_AXIS_DIM:], cos_c, sin_c)], axis=-1)
    return y.astype(x.dtype)


def gqa_attention(q, k, v):
    B, L, H, Dh = q.shape
    Hkv = k.shape[2]
    G = H // Hkv
    nb = L // Q_BLOCK
    scale = Dh ** -0.5
    qb = q.reshape(B, nb, Q_BLOCK, Hkv, G, Dh).transpose(1, 0, 2, 3, 4, 5)

    def block(q_blk):
        s = jnp.einsum('bqkgd,bskd->bkgqs', q_blk, k,
                       preferred_element_type=jnp.float32) * scale
        p = jax.nn.softmax(s, axis=-1).astype(v.dtype)
        return jnp.einsum('bkgqs,bskd->bqkgd', p, v)

    o = lax.map(block, qb)
    return o.transpose(1, 0, 2, 3, 4, 5).reshape(B, L, H * Dh)


def encoder_layer(x, norm_g, w_in, na_q_g, na_k_g, na_rpb, gq_q_g, gq_k_g,
                  w_branch_a, w_branch_b, gate_bias, w_out):
    B, L, _ = x.shape
    h = rmsnorm(x, norm_g)
    proj = h @ w_in
    (na_q, na_k, na_v, na_z, gq_q, gq_k, gq_v, gq_z,
     g_a, g_b) = jnp.split(proj, _split_points(), axis=-1)

    qa = rmsnorm(na_q.reshape(B, L, NA_HEADS, HEAD_DIM), na_q_g)
    ka = rmsnorm(na_k.reshape(B, L, NA_HEADS, HEAD_DIM), na_k_g)
    va = na_v.reshape(B, L, NA_HEADS, HEAD_DIM)
    oa = neighbourhood_attention(qa, ka, va, na_rpb) * jax.nn.silu(na_z)
    pa = oa @ w_branch_a

    tables = axial_rope_tables(L)
    qb = apply_axial_rope(rmsnorm(gq_q.reshape(B, L, GQA_HEADS, HEAD_DIM), gq_q_g), tables)
    kb = apply_axial_rope(rmsnorm(gq_k.reshape(B, L, GQA_KV_HEADS, HEAD_DIM), gq_k_g), tables)
    vb = gq_v.reshape(B, L, GQA_KV_HEADS, HEAD_DIM)
    ob = gqa_attention(qb, kb, vb) * jax.nn.silu(gq_z)
    pb = ob @ w_branch_b

    ga = jax.nn.sigmoid(g_a + gate_bias[:D_MODEL])
    gb = jax.nn.sigmoid(g_b + gate_bias[D_MODEL:])
    merged = ga * pa + gb * pb
    return x + merged @ w_out


def setup_inputs(seed: int = 0) -> dict:
    key = jax.random.key(seed)
    ks = jax.random.split(key, 16)
    f32 = jnp.float32
    nrm = lambda k, shape, s: (jax.random.normal(k, shape, f32) * s).astype(f32)
    return {
        "x_prompt": nrm(ks[0], (BATCH, SEQ, D_MODEL), 1.0),
        "x_sample": nrm(ks[1], (DEC_BATCH, DEC_SEQ, D_MODEL), 1.0),
        "norm_g": 1.0 + nrm(ks[2], (DEPTH, D_MODEL), 0.02),
        "w_in": nrm(ks[3], (DEPTH, D_MODEL, IN_WIDTH), D_MODEL ** -0.5),
        "na_q_g": 1.0 + nrm(ks[4], (DEPTH, HEAD_DIM), 0.02),
        "na_k_g": 1.0 + nrm(ks[5], (DEPTH, HEAD_DIM), 0.02),
        "na_rpb": nrm(ks[6], (DEPTH, NA_HEADS, 2 * NA_KH_MAX - 1, 2 * NA_KW - 1), 0.1),
        "gq_q_g": 1.0 + nrm(ks[7], (DEPTH, HEAD_DIM), 0.02),
        "gq_k_g": 1.0 + nrm(ks[8], (DEPTH, HEAD_DIM), 0.02),
        "w_branch_a": nrm(ks[9], (DEPTH, NA_WIDTH, D_MODEL), NA_WIDTH ** -0.5),
        "w_branch_b": nrm(ks[10], (DEPTH, GQA_WIDTH, D_MODEL), GQA_WIDTH ** -0.5),
        "gate_bias": nrm(ks[11], (DEPTH, 2 * D_MODEL), 0.02),
        "w_out": nrm(ks[12], (DEPTH, D_MODEL, D_MODEL), D_MODEL ** -0.5),
    }


def reference(x_prompt, x_sample, norm_g, w_in, na_q_g, na_k_g, na_rpb, gq_q_g, gq_k_g,
              w_branch_a, w_branch_b, gate_bias, w_out):
    y_prompt = x_prompt
    y_sample = x_sample
    for l in range(DEPTH):
        y_prompt = encoder_layer(y_prompt, norm_g[l], w_in[l], na_q_g[l], na_k_g[l], na_rpb[l],
                                 gq_q_g[l], gq_k_g[l], w_branch_a[l], w_branch_b[l],
                                 gate_bias[l], w_out[l])
        y_sample = encoder_layer(y_sample, norm_g[l], w_in[l], na_q_g[l], na_k_g[l], na_rpb[l],
                                 gq_q_g[l], gq_k_g[l], w_branch_a[l], w_branch_b[l],
                                 gate_bias[l], w_out[l])
    return (y_prompt, y_sample)
```

```python
import numpy as np
from contextlib import ExitStack
import concourse.bass as bass
import concourse.mybir as mybir
from concourse.bass_utils import run_bass_kernel_spmd

F32 = mybir.dt.float32
BF16 = mybir.dt.bfloat16
AF = mybir.ActivationFunctionType
ALU = mybir.AluOpType

D = 2048
NCORES = 8
GRID_W = 64
HD = 128
SCALE = float(HD ** -0.5)
EPS = 1e-6
NEG = -1.0e5
TBL_NEG = -30000.0
SEG_T = 1024
HALO_T = 1536
C_NAQ, C_NAK, C_NAV, C_NAZ = 0, 1024, 2048, 3072
C_GQQ, C_GQK, C_GQV, C_GQZ = 4096, 5120, 5376, 5632
C_GA, C_GB = 6656, 8704
IN_W = 10752
NSLOT = 6
SAME_ENGINE_ALL_SYNC = False
EMBED_WAIT = True


class Chan:
    def __init__(self, sem, inc):
        self.sem, self.inc, self.count = sem, inc, 0


class Buf:
    __slots__ = ("name", "w", "r", "excl")

    def __init__(self, name, excl=False):
        self.name, self.w, self.r, self.excl = name, {}, {}, excl


class Eng:
    def __init__(self, name, handle, chan, is_pe=False):
        self.name, self.h, self.chan, self.is_pe = name, handle, chan, is_pe
        self.seen = {}


class Sched:
    def __init__(self, nc, es):
        self.nc, self.es = nc, es
        self.nsem = 0
        self.pe = Eng("pe", nc.tensor, self.chan(1), True)
        self.act = Eng("act", nc.scalar, self.chan(1))
        self.dve = Eng("dve", nc.vector, self.chan(1))
        self.pool = Eng("pool", nc.gpsimd, self.chan(1))
        self.sp = Eng("sp", nc.sync, None)
        self.out_chans = []
        self.dry = False

    def chan(self, inc=16):
        self.nsem += 1
        sem = self.es.enter_context(self.nc.semaphore("s%d" % self.nsem))
        return Chan(sem, inc)

    def issue(self, eng, fn, reads=(), writes=(), chan=None, inc=True):
        if self.dry:
            return None
        ch = chan if chan is not None else eng.chan
        deps = {}

        def add(ev, raw):
            c, v = ev
            if c is eng.chan:
                if eng.is_pe:
                    return
                if not (raw or SAME_ENGINE_ALL_SYNC):
                    return
            if deps.get(c, 0) < v:
                deps[c] = v

        for b in reads:
            for c, v in b.w.items():
                add((c, v), True)
            if b.excl:
                for c, v in b.r.items():
                    if c is not eng.chan:
                        add((c, v), False)
        for b in writes:
            for c, v in b.w.items():
                add((c, v), False)
            for c, v in b.r.items():
                add((c, v), False)
        need = [(c, v) for c, v in deps.items() if eng.seen.get(c, 0) < v]
        embed = need.pop() if (need and EMBED_WAIT and chan is None) else None
        for c, v in need:
            eng.h.wait_ge(c.sem, v)
            eng.seen[c] = v
        ins = fn()
        if embed is not None:
            ins._wait_ge(embed[0].sem, embed[1])
            eng.seen[embed[0]] = embed[1]
        if inc:
            ch.count += ch.inc
            ins.then_inc(ch.sem, ch.inc)
            val = ch.count
        else:
            val = ch.count + ch.inc
        for b in reads:
            if b.r.get(ch, 0) < val:
                b.r[ch] = val
        for b in writes:
            if b.w.get(ch, 0) < val:
                b.w[ch] = val
            b.r = {}
        return ins

    def fence(self, new_bufs, old_bufs):
        if self.dry:
            return
        ev = {}
        for b in old_bufs:
            for c, v in b.w.items():
                ev[c] = max(ev.get(c, 0), v)
            for c, v in b.r.items():
                ev[c] = max(ev.get(c, 0), v)
        for nb in new_bufs:
            for c, v in ev.items():
                if nb.r.get(c, 0) < v:
                    nb.r[c] = v


def _rope_tables(L):
    t = np.arange(L)
    r = (t // GRID_W).astype(np.float32)
    c = (t % GRID_W).astype(np.float32)
    freqs = (np.float32(10000.0) ** (-np.arange(0, 64, 2, dtype=np.float32) / np.float32(64))).astype(np.float32)
    ang_r = (r[:, None] * freqs[None, :]).astype(np.float32)
    ang_c = (c[:, None] * freqs[None, :]).astype(np.float32)
    cosT = np.empty((128, L), np.float32)
    sinT = np.empty((128, L), np.float32)
    for d in range(128):
        ang = ang_r if d < 64 else ang_c
        f = d % 32
        cosT[d] = np.cos(ang[:, f])
        s = np.sin(ang[:, f])
        sinT[d] = -s if (d % 64) < 32 else s
    return np.stack([cosT, sinT], 0)


def _slot_offs(i):
    if i == 0:
        return [3, 2, 1, 0, -1, -2]
    if i == 7:
        return [2, 1, 0, -1, -2, -3]
    return [2, 1, 0, -1, -2]


def _rowmask(R, r0):
    B = np.full((2, 8, 6, 2), NEG, np.float32)
    for i in range(8):
        for s, off in enumerate(_slot_offs(i)):
            lc = i + 2 + off
            j = r0 // 2 - 2 + lc
            for kr in range(2):
                krg = 2 * j + kr
                for qr in range(2):
                    r = r0 + 2 * i + qr
                    rs = min(max(r - 4, 0), R - 8)
                    if 0 <= krg < R and rs <= krg <= rs + 7:
                        B[kr, i, s, qr] = 0.0
    return B


def _bias_table(rpb, dr_lo=0, dr_hi=14):
    H = rpb.shape[0]
    tbl = np.full((128, H, 16, 64), TBL_NEG, np.float32)
    qc = np.arange(64)
    cs = np.clip(qc - 8, 0, 48)
    kc = np.arange(64)
    inwin = (kc[:, None] >= cs[None, :]) & (kc[:, None] < cs[None, :] + 16)
    dc = np.clip(kc[:, None] - qc[None, :] + 15, 0, 30)
    for kr in range(2):
        for e in range(16):
            dr = 15 - e + kr
            if dr_lo <= dr <= dr_hi:
                g = rpb[:, dr, :][:, dc]
                blk = np.where(inwin[None], g, np.float32(TBL_NEG))
                tbl[kr * 64:(kr + 1) * 64, :, e, :] = blk.transpose(1, 0, 2)
    return tbl


def _halo_rows(x_seq, R, r0):
    rows = np.clip(np.arange(r0 - 4, r0 + 20), 0, R - 1)
    idx = (rows[:, None] * GRID_W + np.arange(GRID_W)[None, :]).reshape(-1)
    return np.ascontiguousarray(x_seq[idx])


def build_program():
    nc = bass.Bass("TRN2", target_bir_lowering=False)

    def din(name, shape, dt=F32):
        return nc.dram_tensor(name, list(shape), dt, kind="ExternalInput").ap()

    xs = din("xs", [2048, D])
    xsh = din("xsh", [2, HALO_T, D])
    xp = din("xp", [8192, D])
    xph = din("xph", [HALO_T, D])
    w_in = din("w_in", [IN_W // 128, 128, 16 * 128])
    w_a = din("w_a", [16, 128, 8 * 128])
    w_b = din("w_b", [16, 128, 8 * 128])
    w_o = din("w_o", [8, 128, 16 * 256])
    gbc_d = din("gbc", [128, D])
    hg_d = din("hg", [128, 4])
    gb_d = din("gbias", [128, 32])
    tbl_d = din("tbl", [128, 8, 2, 1024])
    ident_d = din("ident", [128, 128])
    perm_d = din("perm", [128, 128])
    aind_d = din("aind", [2, 128])
    bmask_d = din("bmask", [2, 3 * 96])
    ropes_d = din("ropes", [2, 128, 2048])
    ropep_d = din("ropep", [2, 128, 8192])
    ropepq_d = din("ropepq", [2, 128, 1024])
    ys = nc.dram_tensor("ys", [2048, D], F32, kind="ExternalOutput").ap()
    yp = nc.dram_tensor("yp", [1024, D], F32, kind="ExternalOutput").ap()
    ksc = [[nc.dram_tensor("ksc%d%d" % (c, g), [128, L], BF16).ap() for g in range(2)]
           for c, L in enumerate((2048, 8192))]
    vsc = [[nc.dram_tensor("vsc%d%d" % (c, g), [128, L // 128, 130], BF16).ap() for g in range(2)]
           for c, L in enumerate((2048, 8192))]

    with ExitStack() as es:
        S = Sched(nc, es)
        PE, ACT, DVE, POOL, SP = S.pe, S.act, S.dve, S.pool, S.sp

        def sb(name, shape, dt):
            return es.enter_context(nc.sbuf_tensor("sb_" + name, list(shape), dt))

        gbc = sb("gbc", [128, D], F32)
        ident = sb("ident", [128, 128], BF16)
        perm = sb("perm", [128, 128], BF16)
        onesb = sb("onesb", [128, 128], BF16)
        hg = sb("hg", [128, 4], F32)
        gbias = sb("gbias", [128, 32], F32)
        aind = sb("aind", [2, 128], BF16)
        bmask = sb("bmask", [2, 3 * 96], BF16)
        epsb = sb("epsb", [128, 1], F32)
        tblbuf = [sb("tbl%d" % i, [128, 1040], F32) for i in range(2)]
        tbl = [t[:, 0:1024] for t in tblbuf]
        hT = sb("hT", [128, 16, HALO_T], BF16)
        oT = sb("oT", [128, 16, SEG_T], BF16)
        xt = [sb("xt%d" % i, [128, D], F32) for i in range(2)]
        xb = [sb("xb%d" % i, [128, D], BF16) for i in range(2)]
        wr = sb("wr", [128, NSLOT * 2048], BF16)
        arena = sb("arena", [128, 18560], BF16)
        pT = sb("pT", [128, 2, 768], BF16)
        tmpf = [sb("tmpf%d" % i, [128, 512], F32) for i in range(4)]
        qnb = [sb("qnb%d" % i, [128, 512], BF16) for i in range(2)]
        ropes = [sb("rope%d" % i, [128, 2, 512], F32) for i in range(2)]
        rc = sb("rc", [128, 8], F32)
        On = [sb("On%d" % i, [128, 4, 128], BF16) for i in range(2)]
        kst = [t[:, 0:512].bitcast(BF16).rearrange("p (g t) -> p g t", g=2) for t in tblbuf]
        vst = [t[:, 512:1032].bitcast(BF16).rearrange("p (g t c) -> p g t c", g=2, t=4) for t in tblbuf]
        yst = [tmpf[i // 2][:, (i % 2) * 256:(i % 2) * 256 + 256] for i in range(4)]
        xres = [tmpf[2 + i // 2][:, (i % 2) * 256:(i % 2) * 256 + 256] for i in range(4)]
        psA = es.enter_context(nc.psum_tensor("psA", [128, 6, 512], F32))
        psT = es.enter_context(nc.psum_tensor("psT", [128, 2, 8, 128], BF16))
        bA = [Buf("psA%d" % i, excl=True) for i in range(6)]
        bT = [Buf("psT%d" % i, excl=True) for i in range(2)]

        def av(off, shape):
            n = int(np.prod(shape))
            ap = arena[:, off:off + n]
            if len(shape) == 2:
                return ap.rearrange("p (a b) -> p a b", a=shape[0])
            return ap

        PH = 5144
        ph = []
        for i in range(2):
            o = i * PH
            ph.append(dict(qT=av(o, [1024]), kT=av(o + 1024, [1536]), vn=av(o + 2560, [12, 130]),
                           zT=av(o + 4120, [1024]),
                           b_q=Buf("qT%d" % i), b_k=Buf("kT%d" % i), b_v=Buf("vn%d" % i), b_z=Buf("zT%d" % i)))
        CR0 = 2 * PH
        ctxr = []
        for i in range(4):
            o = CR0 + i * 2064
            ctxr.append(dict(kT=av(o, [1024]), va=av(o + 1024, [8, 130]), buf=Buf("ctx%d" % i), chan=S.chan(),
                             tag=None))
        mT = arena[:, 0:16384].rearrange("p (a b) -> p a b", a=16)
        b_mT = Buf("mT")
        arena_att_bufs = [d[k] for d in ph for k in ("b_q", "b_k", "b_v", "b_z")] + [c["buf"] for c in ctxr]

        b_const = Buf("const")
        b_tbl = [Buf("tbl0"), Buf("tbl1")]
        b_hT = Buf("hT")
        b_oT = [Buf("oT0"), Buf("oT1")]
        b_xt = [Buf("xt0"), Buf("xt1")]
        b_xb = [Buf("xb0"), Buf("xb1")]
        b_pTa = [Buf("pTa0"), Buf("pTa1")]
        b_pTb = [Buf("pTb0"), Buf("pTb1")]
        b_pT = b_pTa + b_pTb
        b_pT3 = [Buf("pT3_%d" % i) for i in range(3)]
        b_tmp = [Buf("tmp%d" % i) for i in range(4)]
        b_qnb = [Buf("qnb0"), Buf("qnb1")]
        b_rope = [Buf("rope0"), Buf("rope1")]
        b_ss = [Buf("ss0"), Buf("ss1")]
        b_rstd = [Buf("rstd0"), Buf("rstd1")]
        b_rc = Buf("rc")
        b_On = [Buf("On0"), Buf("On1")]
        b_kst = b_tbl
        b_vst = b_tbl
        b_yst = [Buf("yst%d" % i) for i in range(4)]
        b_xres = [Buf("xres%d" % i) for i in range(4)]
        out_bufs_alias = b_yst + b_xres
        b_wr = [Buf("wr%d" % i) for i in range(NSLOT)]
        b_ksc = [[Buf("ksc%d%d" % (c, g)) for g in range(2)] for c in range(2)]

        ch_const = {"sp": S.chan(), "pool": S.chan()}
        ch_xt = [S.chan(), S.chan()]
        ch_tbl = [S.chan(), S.chan()]
        ch_rope = [S.chan(), S.chan()]
        ch_wr = [S.chan() for _ in range(NSLOT)]
        ch_kst = [S.chan(), S.chan()]
        ch_xres = [S.chan() for _ in range(4)]
        ch_yst = [S.chan() for _ in range(4)]
        S.out_chans = ch_yst

        pT_flat = pT[:, :, :].rearrange("p a b -> p (a b)")
        pT3 = [pT_flat[:, j * 512:(j + 1) * 512] for j in range(3)]
        S3 = [psA[:, 0, :], psA[:, 1, :], psT[:, 1, :, :].rearrange("p a b -> p (a b)").bitcast(F32)]
        bS3 = [bA[0], bA[1], bT[1]]
        hTc = [oT[:, 0:8, :].rearrange("p a b -> p (a b)").rearrange("p (a b) -> p a b", a=16),
               oT[:, 8:16, :].rearrange("p a b -> p (a b)").rearrange("p (a b) -> p a b", a=16)]

        def cload(q, dst, src):
            S.issue(q, lambda: q.h.dma_start(out=dst, in_=src), writes=[b_const], chan=ch_const[q.name])

        cload(SP, gbc[:], gbc_d)
        cload(SP, hg[:], hg_d)
        cload(SP, gbias[:], gb_d)
        cload(POOL, ident[:], ident_d)
        cload(POOL, perm[:], perm_d)
        cload(POOL, aind[:], aind_d)
        cload(POOL, bmask[:], bmask_d)
        S.issue(POOL, lambda: nc.gpsimd.memset(onesb[:], 1.0 / 128.0), writes=[b_const])
        S.issue(POOL, lambda: nc.gpsimd.memset(epsb[:], EPS), writes=[b_const])

        ring = {"free": list(range(NSLOT)), "plan": [], "pos": 0, "next": 0, "loaded": {}}
        LOOKAHEAD = 8

        def _try_load(idx):
            src_ap, kchunks, ncols = ring["plan"][idx]
            nsl = max(1, (kchunks * ncols) // 2048)
            start = None
            for s0 in sorted(ring["free"]):
                if nsl == 2 and s0 % 2:
                    continue
                if all((s0 + j) in ring["free"] for j in range(nsl)):
                    start = s0
                    break
            if start is None:
                return False
            slots = list(range(start, start + nsl))
            for s_ in slots:
                ring["free"].remove(s_)
            view = wr[:, start * 2048:start * 2048 + kchunks * ncols].rearrange("p (k c) -> p k c", k=kchunks)
            bufs = [b_wr[s_] for s_ in slots]
            if isinstance(src_ap, (list, tuple)):
                nb_ = len(src_ap)
                for bi, sa in enumerate(src_ap):
                    S.issue(POOL, lambda bi=bi, sa=sa: nc.gpsimd.dma_start(out=view[:, :, bi * 128:(bi + 1) * 128],
                                                                            in_=sa.rearrange("p (k c) -> p k c", k=kchunks)),
                            writes=bufs, chan=ch_wr[start])
            else:
                S.issue(POOL, lambda: nc.gpsimd.dma_start(out=view, in_=src_ap.rearrange("p (k c) -> p k c", k=kchunks)),
                        writes=bufs, chan=ch_wr[start])
            ring["loaded"][idx] = (view, bufs, slots)
            return True

        def _pump():
            while ring["next"] < len(ring["plan"]) and ring["next"] < ring["pos"] + LOOKAHEAD:
                if not _try_load(ring["next"]):
                    break
                ring["next"] += 1

        def wload(src_ap, kchunks, ncols):
            if S.dry:
                ring["plan"].append((src_ap, kchunks, ncols))
                view = wr[:, 0:kchunks * ncols].rearrange("p (k c) -> p k c", k=kchunks)
                return view, [], []
            idx = ring["pos"]
            ring["pos"] += 1
            while ring["next"] <= idx:
                ok = _try_load(ring["next"])
                assert ok, "weight ring exhausted"
                ring["next"] += 1
            res = ring["loaded"].pop(idx)
            _pump()
            return res

        def wfree(slots):
            if S.dry:
                return
            ring["free"].extend(slots)
            _pump()

        state = {"xi": 0, "ti": 0, "ri": 0, "pi": 0, "tbi": 0, "oni": 0, "ki": 0, "yi": 0}

        hT_flat = hT[:, :, :].rearrange("p a b -> p (a b)")
        NXB = 4
        stat = sb("stat", [128, 2 * (2 + NXB)], F32)
        xpool = []
        for i in range(2 + NXB):
            if i < 2:
                xt_ap, xb_ap = xt[i][:], xb[i][:]
            else:
                j = i - 2
                xt_ap = hT_flat[:, j * 4096:(j + 1) * 4096].bitcast(F32)
                xb_ap = hT_flat[:, 16384 + j * 2048:16384 + (j + 1) * 2048]
            xpool.append(dict(xt=xt_ap, xb=xb_ap, b_xt=Buf("xt%d" % i), b_xb=Buf("xb%d" % i), ch=S.chan(),
                              ss=stat[:, i:i + 1], rs=stat[:, 2 + NXB + i:3 + NXB + i], b_ss=Buf("ss%d" % i), b_rs=Buf("rs%d" % i)))
        xpool_extra_bufs = [p_[k_] for p_ in xpool[2:] for k_ in ("b_xt", "b_xb")]

        class TileStream:
            def __init__(self, tiles, pool):
                self.tiles, self.pool, self.nl, self.nc_ = tiles, pool, 0, 0

            def pump(self, upto):
                while self.nl < min(len(self.tiles), upto + 1):
                    k = self.nl
                    pb = self.pool[k % len(self.pool)]
                    x_ap = self.tiles[k][0]
                    S.issue(SP, lambda: nc.sync.dma_start(out=pb["xt"], in_=x_ap), writes=[pb["b_xt"]], chan=pb["ch"])
                    self.nl += 1

            def phase_a(self, k):
                self.pump(k + len(self.pool) - 1)
                pb = self.pool[k % len(self.pool)]
                S.issue(POOL, lambda: nc.gpsimd.memset(pb["ss"], 0.0), writes=[pb["b_ss"]])
                S.issue(ACT, lambda: nc.scalar.activation(out=pb["xb"], in_=pb["xt"], func=AF.Square,
                                                          scale=float(1.0 / np.sqrt(D)), accum_out=pb["ss"]),
                        reads=[pb["b_xt"], pb["b_ss"]], writes=[pb["b_xb"], pb["b_ss"]])
                S.issue(ACT, lambda: nc.scalar.activation(out=pb["rs"], in_=pb["ss"], func=AF.Ln, bias=epsb[:]),
                        reads=[pb["b_ss"], b_const], writes=[pb["b_rs"]])
                S.issue(ACT, lambda: nc.scalar.activation(out=pb["rs"], in_=pb["rs"], func=AF.Exp, scale=-0.5),
                        reads=[pb["b_rs"]], writes=[pb["b_rs"]])
                S.issue(DVE, lambda: nc.vector.scalar_tensor_tensor(out=pb["xb"], in0=pb["xt"], scalar=pb["rs"],
                                                                    in1=gbc[:], op0=ALU.mult, op1=ALU.mult),
                        reads=[pb["b_xt"], pb["b_rs"], b_const], writes=[pb["b_xb"]])

            def phase_b(self, k):
                pb = self.pool[k % len(self.pool)]
                _, dst, dst_buf = self.tiles[k]
                for hf in range(2):
                    for kk in range(8):
                        c0 = (hf * 8 + kk) * 128
                        S.issue(PE, lambda: nc.tensor.transpose(psT[:, hf, kk, :], pb["xb"][:, c0:c0 + 128], ident[:]),
                                reads=[pb["b_xb"], b_const], writes=[bT[hf]], inc=(kk == 7))
                    if hf == 0:
                        S.issue(ACT, lambda: nc.scalar.copy(out=dst[:, 0:8, :], in_=psT[:, 0, :, :]), reads=[bT[0]], writes=[dst_buf])
                    else:
                        S.issue(DVE, lambda: nc.vector.tensor_copy(out=dst[:, 8:16, :], in_=psT[:, 1, :, :]), reads=[bT[1]], writes=[dst_buf])

            def compute_next(self):
                k = self.nc_
                self.nc_ += 1
                if k == 0:
                    self.phase_a(0)
                if k + 1 < len(self.tiles):
                    self.phase_a(k + 1)
                self.phase_b(k)

            def finish(self):
                while self.nc_ < len(self.tiles):
                    self.compute_next()

        def proj_fm(wv, wbufs, src, src_buf, t0, nt, bank):
            for k in range(16):
                S.issue(PE, lambda k=k: nc.tensor.matmul(psA[:, bank, 0:nt], lhsT=wv[:, k, :], rhs=src[:, k, t0:t0 + nt],
                                                         start=(k == 0), stop=(k == 15)),
                        reads=wbufs + [src_buf], writes=[bA[bank]], inc=(k == 15))

        def rope_load(tab_d, p0, nt):
            j = state["ri"]
            state["ri"] ^= 1
            S.issue(SP, lambda: nc.sync.dma_start(out=ropes[j][:, :, 0:nt], in_=tab_d[:, :, p0:p0 + nt].rearrange("a p t -> p a t")),
                    writes=[b_rope[j]], chan=ch_rope[j])
            return j

        SETS = [dict(q=0, ms=2, pm=4, tA=tmpf[0], tC=tmpf[1], yb=qnb[0], bA_=b_tmp[0], bC_=b_tmp[1], bY=b_qnb[0]),
                dict(q=1, ms=3, pm=5, tA=tmpf[2], tC=tmpf[3], yb=qnb[1], bA_=b_tmp[2], bC_=b_tmp[3], bY=b_qnb[1])]

        def blk_s1(B, st):
            kind = B["kind"]
            q = st["q"]
            if kind in ("norm", "silu", "vT"):
                proj_fm(B["w"][0], B["w"][1], B["src"], B["src_buf"], B["t0"], 512, q)
            if kind == "vT":
                S.issue(ACT, lambda: nc.scalar.copy(out=st["yb"][:], in_=psA[:, q, :]), reads=[bA[q]], writes=[st["bY"]])
            if kind == "norm":
                g = B["g"]
                rope_j = B.get("rope")
                S.issue(ACT, lambda: nc.scalar.activation(out=st["tA"][:, 0:256].bitcast(BF16), in_=psA[:, q, :], func=AF.Square),
                        reads=[bA[q]], writes=[st["bA_"]])
                if rope_j is not None:
                    S.issue(ACT, lambda: nc.scalar.activation(out=st["yb"][:], in_=psA[:, q, :], func=AF.Copy, scale=hg[:, g:g + 1]),
                            reads=[bA[q], b_const], writes=[st["bY"]])
                    S.issue(DVE, lambda: nc.vector.scalar_tensor_tensor(out=st["tC"][:], in0=psA[:, q, :], scalar=hg[:, g:g + 1],
                                                                        in1=ropes[rope_j][:, 0, :], op0=ALU.mult, op1=ALU.mult),
                            reads=[bA[q], b_rope[rope_j], b_const], writes=[st["bC_"]])
                else:
                    S.issue(DVE, lambda: nc.vector.tensor_scalar(out=st["tC"][:], in0=psA[:, q, :], scalar1=hg[:, g:g + 1], scalar2=None,
                                                                 op0=ALU.mult),
                            reads=[bA[q], b_const], writes=[st["bC_"]])
            elif kind == "silu":
                S.issue(ACT, lambda: nc.scalar.activation(out=B["out"], in_=psA[:, q, :], func=AF.Silu),
                        reads=[bA[q]], writes=[B["out_buf"]])
            elif kind == "vtm":
                nt_, ncol = B["ntiles"], B["ncols"]
                for t in range(nt_):
                    tk = B["t0"] + t * 128
                    for k in range(16):
                        S.issue(PE, lambda k=k, t=t, tk=tk: nc.tensor.matmul(psA[:, q, t * ncol:(t + 1) * ncol], lhsT=B["src"][:, k, tk:tk + 128],
                                                                            rhs=B["w"][0][:, k, :], start=(k == 0), stop=(k == 15)),
                                reads=B["w"][1] + [B["src_buf"]], writes=[bA[q]], inc=(k == 15))
                B["evac"](q)

        def blk_s2(B, st):
            if B["kind"] == "vT":
                tb = st["q"] % 2
                for t in range(4):
                    S.issue(PE, lambda t=t: nc.tensor.transpose(psT[:, tb, t, :], st["yb"][:, t * 128:(t + 1) * 128], ident[:]),
                            reads=[st["bY"], b_const], writes=[bT[tb]], inc=(t == 3))
                S.issue(DVE, lambda: nc.vector.tensor_copy(out=B["out"], in_=psT[:, tb, 0:4, :]), reads=[bT[tb]], writes=[B["out_buf"]])
                return
            if B["kind"] != "norm":
                return
            ms, pm = st["ms"], st["pm"]
            rope_j = B.get("rope")
            S.issue(PE, lambda: nc.tensor.matmul(psA[:, ms, :], lhsT=onesb[:], rhs=st["tA"][:, 0:256].bitcast(BF16), start=True, stop=True),
                    reads=[st["bA_"], b_const], writes=[bA[ms]])
            if rope_j is not None:
                S.issue(PE, lambda: nc.tensor.matmul(psA[:, pm, :], lhsT=perm[:], rhs=st["yb"][:], start=True, stop=True),
                        reads=[st["bY"], b_const], writes=[bA[pm]])
            S.issue(ACT, lambda: nc.scalar.activation(out=psA[:, ms, :], in_=psA[:, ms, :], func=AF.Ln, bias=epsb[:]),
                    reads=[bA[ms], b_const], writes=[bA[ms]])
            S.issue(ACT, lambda: nc.scalar.activation(out=psA[:, ms, :], in_=psA[:, ms, :], func=AF.Exp, scale=-0.5),
                    reads=[bA[ms]], writes=[bA[ms]])
            if rope_j is not None:
                S.issue(DVE, lambda: nc.vector.tensor_tensor(out=st["tA"][:], in0=psA[:, pm, :], in1=ropes[rope_j][:, 1, :], op=ALU.mult),
                        reads=[bA[pm], b_rope[rope_j]], writes=[st["bA_"]])
                S.issue(DVE, lambda: nc.vector.tensor_tensor(out=st["tC"][:], in0=st["tC"][:], in1=st["tA"][:], op=ALU.add),
                        reads=[st["bC_"], st["bA_"]], writes=[st["bC_"]])
            S.issue(DVE, lambda: nc.vector.tensor_tensor(out=B["out"], in0=st["tC"][:], in1=psA[:, ms, :], op=ALU.mult),
                    reads=[st["bC_"], bA[ms]], writes=[B["out_buf"]])

        def run_blocks_gen(blocks):
            for idx, B in enumerate(blocks):
                blk_s1(B, SETS[idx % 2])
                if idx >= 1:
                    blk_s2(blocks[idx - 1], SETS[(idx - 1) % 2])
                yield
            if blocks:
                blk_s2(blocks[-1], SETS[(len(blocks) - 1) % 2])

        def run_blocks(blocks):
            for _ in run_blocks_gen(blocks):
                pass

        def ctx_prepass(ci, x_ctx, C, rope_tab):
            wk = [wload(w_in[C_GQK // 128 + g], 16, 128) for g in range(2)]
            wvv = [wload(w_in[C_GQV // 128 + g], 16, 128) for g in range(2)]
            for i in range(2):
                S.issue(POOL, lambda i=i: nc.gpsimd.memset(vst[i], 1.0), writes=[b_vst[i]])
            nb = C // 512
            S.fence(xpool_extra_bufs, [b_hT])
            tiles = []
            for blk in range(nb):
                hb = blk % 2
                for t in range(4):
                    tok = blk * 512 + t * 128
                    tiles.append((x_ctx[tok:tok + 128, :], hTc[hb][:, :, t * 128:(t + 1) * 128], b_oT[hb]))
            ts = TileStream(tiles, xpool)
            pending_store = []

            def flush_store(upto=None):
                while pending_store and (upto is None or pending_store[0][0] <= upto):
                    blk_, si_ = pending_store.pop(0)
                    for g in range(2):
                        S.issue(SP, lambda g=g: nc.sync.dma_start(out=ksc[ci][g][:, blk_ * 512:(blk_ + 1) * 512], in_=kst[si_][:, g, :]),
                                reads=[b_kst[si_]], writes=[b_ksc[ci][g]], chan=ch_kst[si_])
                        S.issue(SP, lambda g=g: nc.sync.dma_start(out=vsc[ci][g][:, blk_ * 4:(blk_ + 1) * 4, :], in_=vst[si_][:, g, :, :]),
                                reads=[b_vst[si_]], writes=[b_ksc[ci][g]], chan=ch_kst[si_])

            for t in range(4):
                ts.compute_next()
            for blk in range(nb):
                hb = blk % 2
                flush_store(blk - 2)
                j = rope_load(rope_tab, blk * 512, 512)
                si = state["ki"]
                state["ki"] ^= 1
                blocks = []
                for g in range(2):
                    blocks.append(dict(kind="norm", w=wk[g], src=hTc[hb], src_buf=b_oT[hb], t0=0, g=3, rope=j,
                                       out=kst[si][:, g, :], out_buf=b_kst[si]))
                for g in range(2):
                    blocks.append(dict(kind="vT", w=wvv[g], src=hTc[hb], src_buf=b_oT[hb], t0=0,
                                       out=vst[si][:, g, :, 0:128], out_buf=b_vst[si]))
                gen = run_blocks_gen(blocks)
                for t in range(4):
                    if blk + 1 < nb:
                        ts.compute_next()
                    next(gen, None)
                for _ in gen:
                    pass
                pending_store.append((blk, si))
            flush_store()
            S.fence([b_hT], xpool_extra_bufs)
            for w_ in wk + wvv:
                wfree(w_[2])

        def ctx_block(ci, g, blk):
            tag = (ci, g, blk)
            for c in ctxr:
                if c["tag"] == tag:
                    return c
            c = ctxr[state["ti"] % 4]
            state["ti"] += 1
            c["tag"] = tag
            S.issue(SP, lambda: nc.sync.dma_start(out=c["kT"], in_=ksc[ci][g][:, blk * 1024:(blk + 1) * 1024]),
                    reads=[b_ksc[ci][g]], writes=[c["buf"]], chan=c["chan"])
            S.issue(SP, lambda: nc.sync.dma_start(out=c["va"], in_=vsc[ci][g][:, blk * 8:(blk + 1) * 8, :]),
                    reads=[b_ksc[ci][g]], writes=[c["buf"]], chan=c["chan"])
            return c

        def finish_o(rc_src, o_src, banks, hidx, q0, P, tb=None):
            oi = state["oni"]
            state["oni"] ^= 1
            if tb is None:
                tb = oi
            n = len(banks)
            bset = sorted(set(bk for bk, _ in banks))
            S.issue(DVE, lambda: nc.vector.reciprocal(out=rc[:, 0:n], in_=rc_src), reads=[bA[b_] for b_ in bset], writes=[b_rc])
            S.issue(DVE, lambda: nc.vector.tensor_tensor(out=On[oi][:, 0:n, :], in0=o_src, in1=rc[:, 0:n].unsqueeze(2).to_broadcast([128, n, 128]),
                                                         op=ALU.mult),
                    reads=[bA[b_] for b_ in bset] + [b_rc], writes=[b_On[oi]])
            for t in range(n):
                S.issue(PE, lambda t=t: nc.tensor.transpose(psT[:, tb, t, :], On[oi][:, t, :], ident[:]),
                        reads=[b_On[oi], b_const], writes=[bT[tb]], inc=(t == n - 1))
            S.issue(DVE, lambda: nc.vector.tensor_tensor(out=oT[:, hidx, q0:q0 + n * 128].rearrange("p (a b) -> p a b", a=n),
                                                         in0=psT[:, tb, 0:n, :],
                                                         in1=P["zT"][:, q0:q0 + n * 128].rearrange("p (a b) -> p a b", a=n), op=ALU.mult),
                    reads=[bT[tb], P["b_z"]], writes=[b_oT[hidx // 8]])

        def stage_a_gen(xh):
            ts = TileStream([(xh[t * 128:(t + 1) * 128, :], hT[:, :, t * 128:(t + 1) * 128], b_hT) for t in range(12)], xpool[0:2])
            ts.pump(1)
            yield
            for _ in range(12):
                ts.compute_next()
                yield

        def segment(si, xh, ci, C, rope_ctx, rope_q, q_off, y_out, stage_a_done=False, next_stage_a=None):
            S.fence(arena_att_bufs, [b_mT])
            for c in ctxr:
                c["tag"] = None
            if not stage_a_done:
                for _ in stage_a_gen(xh):
                    pass
            for P in ph:
                S.issue(POOL, lambda P=P: nc.gpsimd.memset(P["vn"][:, :, 128:130], 1.0), writes=[P["b_v"]])
            S.fence(b_pT3, b_pT)
            hcount = 0
            for g in range(2):
                for hq in range(4):
                    head = 4 * g + hq
                    P = ph[hcount % 2]
                    hcount += 1
                    wq = wload(w_in[C_GQQ // 128 + head], 16, 128)
                    wz = wload(w_in[C_GQZ // 128 + head], 16, 128)
                    blocks = []
                    for qb in range(2):
                        j = rope_load(rope_q, q_off + qb * 512, 512)
                        blocks.append(dict(kind="norm", w=wq, src=hT, src_buf=b_hT, t0=256 + qb * 512, g=2, rope=j,
                                           out=P["qT"][:, qb * 512:(qb + 1) * 512], out_buf=P["b_q"]))
                    for qb in range(2):
                        blocks.append(dict(kind="silu", w=wz, src=hT, src_buf=b_hT, t0=256 + qb * 512,
                                           out=P["zT"][:, qb * 512:(qb + 1) * 512], out_buf=P["b_z"]))
                    run_blocks(blocks)
                    wfree(wq[2]); wfree(wz[2])
                    nblk = C // 1024
                    chunks = [(qb, blk, kc) for qb in range(2) for blk in range(nblk) for kc in range(8)]
                    n = len(chunks)
                    nper = nblk * 8
                    cblk = {}

                    def emit_S(idx):
                        qb, blk, kc = chunks[idx]
                        c = ctx_block(ci, g, blk)
                        if kc == 0 and blk + 1 < nblk:
                            ctx_block(ci, g, blk + 1)
                        cblk[idx] = c
                        j3 = idx % 3
                        S.issue(PE, lambda: nc.tensor.matmul(S3[j3], lhsT=c["kT"][:, kc * 128:(kc + 1) * 128],
                                                             rhs=P["qT"][:, qb * 512:(qb + 1) * 512], start=True, stop=True),
                                reads=[c["buf"], P["b_q"]], writes=[bS3[j3]])
                        S.issue(ACT, lambda: nc.scalar.activation(out=pT3[j3], in_=S3[j3], func=AF.Exp, scale=SCALE),
                                reads=[bS3[j3]], writes=[b_pT3[j3]])

                    def emit_PV(idx):
                        qb, blk, kc = chunks[idx]
                        c = cblk.pop(idx)
                        j3 = idx % 3
                        first, last = (idx % nper == 0), (idx % nper == nper - 1)
                        for t in range(4):
                            S.issue(PE, lambda t=t: nc.tensor.matmul(psA[:, 2 + t, 0:129], lhsT=pT3[j3][:, t * 128:(t + 1) * 128],
                                                                     rhs=c["va"][:, kc, 0:129], start=first, stop=last),
                                    reads=[b_pT3[j3], c["buf"]], writes=[bA[2 + t]], inc=(last or t == 3))
                        if last:
                            finish_o(psA[:, 2:6, 128], psA[:, 2:6, 0:128], [(2 + t, 0) for t in range(4)], 8 + head, qb * 512, P, tb=0)

                    emit_S(0)
                    emit_S(1)
                    for idx in range(n):
                        if idx + 2 < n:
                            emit_S(idx + 2)
                        emit_PV(idx)
            S.fence(b_pT, b_pT3)
            for h in range(8):
                P = ph[h % 2]
                for v_ in range(2):
                    S.issue(SP, lambda v_=v_: nc.sync.dma_start(out=tbl[v_], in_=tbl_d[:, h, v_, :]), writes=[b_tbl[v_]], chan=ch_tbl[v_])
                edge_q = {0: (0, 1), 1: (6, 7), 2: (0, 1, 6, 7)}[si]
                wq = wload(w_in[C_NAQ // 128 + h], 16, 128)
                wk = wload(w_in[C_NAK // 128 + h], 16, 128)
                wv = wload(w_in[C_NAV // 128 + h], 16, 128)
                wz = wload(w_in[C_NAZ // 128 + h], 16, 128)
                blocks = []
                for qb in range(2):
                    blocks.append(dict(kind="norm", w=wq, src=hT, src_buf=b_hT, t0=256 + qb * 512, g=0,
                                       out=P["qT"][:, qb * 512:(qb + 1) * 512], out_buf=P["b_q"]))
                for kb in range(3):
                    blocks.append(dict(kind="norm", w=wk, src=hT, src_buf=b_hT, t0=kb * 512, g=1,
                                       out=P["kT"][:, kb * 512:(kb + 1) * 512], out_buf=P["b_k"]))
                for vb in range(3):
                    blocks.append(dict(kind="vT", w=wv, src=hT, src_buf=b_hT, t0=vb * 512,
                                       out=P["vn"][:, vb * 4:(vb + 1) * 4, 0:128], out_buf=P["b_v"]))
                for qb in range(2):
                    blocks.append(dict(kind="silu", w=wz, src=hT, src_buf=b_hT, t0=256 + qb * 512,
                                       out=P["zT"][:, qb * 512:(qb + 1) * 512], out_buf=P["b_z"]))
                run_blocks(blocks)
                for w_ in (wq, wk, wv, wz):
                    wfree(w_[2])
                def na_S(i):
                    offs = _slot_offs(i)
                    nsl = len(offs)
                    rg = (i % 2) * 2
                    pb = i % 2
                    edge = i in edge_q
                    ti = 1 if edge else 0
                    e0 = 8 - 2 * offs[0]
                    for half, (s0, s1, bk, bpt) in enumerate(((0, 3, rg, b_pTa[pb]), (3, nsl, rg + 1, b_pTb[pb]))):
                        for s_ in range(s0, s1):
                            lc = i + 2 + offs[s_]
                            c0 = (s_ - s0) * 128
                            grp_end = (s_ == s1 - 1)
                            S.issue(PE, lambda: nc.tensor.matmul(psA[:, bk, c0:c0 + 128], lhsT=P["kT"][:, lc * 128:(lc + 1) * 128],
                                                                 rhs=P["qT"][:, i * 128:(i + 1) * 128], start=True, stop=(not edge)),
                                    reads=[P["b_k"], P["b_q"]], writes=[bA[bk]], inc=(grp_end and not edge))
                            if edge:
                                mi = si * 96 + (i * 6 + s_) * 2
                                S.issue(PE, lambda: nc.tensor.matmul(psA[:, bk, c0:c0 + 128], lhsT=aind[:, :],
                                                                     rhs=bmask[:, mi:mi + 2].unsqueeze(2).to_broadcast([2, 2, 64]),
                                                                     start=False, stop=True),
                                        reads=[b_const], writes=[bA[bk]], inc=grp_end)
                        w_ = (s1 - s0) * 128
                        S.issue(DVE, lambda: nc.vector.scalar_tensor_tensor(out=psA[:, bk, 0:w_], in0=psA[:, bk, 0:w_], scalar=SCALE,
                                                                            in1=tbl[ti][:, (e0 + 2 * s0) * 64:(e0 + 2 * s1) * 64],
                                                                            op0=ALU.mult, op1=ALU.add),
                                reads=[bA[bk], b_tbl[ti]], writes=[bA[bk]])
                        S.issue(ACT, lambda: nc.scalar.activation(out=pT[:, pb, s0 * 128:s1 * 128], in_=psA[:, bk, 0:w_], func=AF.Exp),
                                reads=[bA[bk]], writes=[bpt])

                def na_PV(i):
                    offs = _slot_offs(i)
                    nsl = len(offs)
                    pb = i % 2
                    ob = 4 + (i // 2) % 2
                    oc = (i % 2) * 256
                    for s_, off in enumerate(offs):
                        lc = i + 2 + off
                        bpt = b_pTa[pb] if s_ < 3 else b_pTb[pb]
                        S.issue(PE, lambda: nc.tensor.matmul(psA[:, ob, oc:oc + 129], lhsT=pT[:, pb, s_ * 128:(s_ + 1) * 128],
                                                             rhs=P["vn"][:, lc, 0:129], start=(s_ == 0), stop=(s_ == nsl - 1)),
                                reads=[bpt, P["b_v"]], writes=[bA[ob]], inc=(s_ == nsl - 1 or s_ == 2))
                    if i % 2 == 1:
                        finish_o(psA[:, ob, :].rearrange("p (a b) -> p a b", a=2)[:, :, 128],
                                 psA[:, ob, :].rearrange("p (a b) -> p a b", a=2)[:, :, 0:128], [(ob, 0), (ob, 256)], h, (i - 1) * 128, P)

                na_S(0)
                for i in range(8):
                    if i + 1 < 8:
                        na_S(i + 1)
                    na_PV(i)
            S.fence([b_mT], arena_att_bufs)
            if next_stage_a is not None:
                next(next_stage_a, None)
            for cb in range(16):
                wga = wload(w_in[C_GA // 128 + cb], 16, 128)
                wgb = wload(w_in[C_GB // 128 + cb], 16, 128)
                wa = wload(w_a[cb], 8, 128)
                wb = wload(w_b[cb], 8, 128)
                for tb in range(2):
                    t0 = tb * 512
                    proj_fm(wga[0], wga[1], hT, b_hT, 256 + t0, 512, 0)
                    proj_fm(wgb[0], wgb[1], hT, b_hT, 256 + t0, 512, 1)
                    for br, (wx, bank) in enumerate(((wa, 2), (wb, 3))):
                        for k in range(8):
                            S.issue(PE, lambda k=k, wx=wx, bank=bank, br=br: nc.tensor.matmul(psA[:, bank, :], lhsT=wx[0][:, k, :],
                                                                                             rhs=oT[:, br * 8 + k, t0:t0 + 512], start=(k == 0), stop=(k == 7)),
                                    reads=wx[1] + [b_oT[br]], writes=[bA[bank]], inc=(k == 7))
                    S.issue(ACT, lambda: nc.scalar.activation(out=tmpf[0][:], in_=psA[:, 0, :], func=AF.Sigmoid, bias=gbias[:, cb:cb + 1]),
                            reads=[bA[0], b_const], writes=[b_tmp[0]])
                    S.issue(ACT, lambda: nc.scalar.activation(out=tmpf[1][:], in_=psA[:, 1, :], func=AF.Sigmoid, bias=gbias[:, 16 + cb:17 + cb]),
                            reads=[bA[1], b_const], writes=[b_tmp[1]])
                    S.issue(DVE, lambda: nc.vector.tensor_tensor(out=tmpf[2][:], in0=psA[:, 2, :], in1=tmpf[0][:], op=ALU.mult),
                            reads=[bA[2], b_tmp[0]], writes=[b_tmp[2]])
                    S.issue(DVE, lambda: nc.vector.tensor_tensor(out=tmpf[3][:], in0=psA[:, 3, :], in1=tmpf[1][:], op=ALU.mult),
                            reads=[bA[3], b_tmp[1]], writes=[b_tmp[3]])
                    S.issue(POOL, lambda: nc.gpsimd.tensor_tensor(out=mT[:, cb, t0:t0 + 512], in0=tmpf[2][:], in1=tmpf[3][:], op=ALU.add),
                            reads=[b_tmp[2], b_tmp[3]], writes=[b_mT])
                for w_ in (wga, wgb, wa, wb):
                    wfree(w_[2])
            S.fence(out_bufs_alias, b_tmp)
            oblks = [(ob, tt) for ob in range(8) for tt in range(8)]

            def xres_load(n):
                ob, tt = oblks[n]
                S.issue(SP, lambda: nc.sync.dma_start(out=xres[n % 4], in_=xh[256 + tt * 128:256 + (tt + 1) * 128, ob * 256:(ob + 1) * 256]),
                        writes=[b_xres[n % 4]], chan=ch_xres[n % 4])

            for n in range(3):
                xres_load(n)
            wo = None
            for n, (ob, tt) in enumerate(oblks):
                if tt == 0:
                    wo = wload(w_o[ob], 16, 256)
                if n + 3 < len(oblks):
                    xres_load(n + 3)
                yi = n % 4
                bank = 4 + (n % 2)
                for k in range(16):
                    S.issue(PE, lambda k=k: nc.tensor.matmul(psA[:, bank, 0:256], lhsT=mT[:, k, tt * 128:(tt + 1) * 128], rhs=wo[0][:, k, :],
                                                             start=(k == 0), stop=(k == 15)),
                            reads=wo[1] + [b_mT], writes=[bA[bank]], inc=(k == 15))
                S.issue(DVE, lambda: nc.vector.tensor_tensor(out=yst[yi], in0=psA[:, bank, 0:256], in1=xres[yi], op=ALU.add),
                        reads=[bA[bank], b_xres[yi]], writes=[b_yst[yi]])
                S.issue(SP, lambda: nc.sync.dma_start(out=y_out[tt * 128:(tt + 1) * 128, ob * 256:(ob + 1) * 256], in_=yst[yi]),
                        reads=[b_yst[yi]], chan=ch_yst[yi])
                if next_stage_a is not None and n % 5 == 2:
                    next(next_stage_a, None)
                if tt == 7:
                    wfree(wo[2])
            S.fence(b_tmp, out_bufs_alias)
            if next_stage_a is not None:
                for _ in next_stage_a:
                    pass

        def program():
            ctx_prepass(0, xs, 2048, ropes_d)
            ctx_prepass(1, xp, 8192, ropep_d)
            segment(0, xsh[0], 0, 2048, ropes_d, ropes_d, 0, ys[0:1024, :], next_stage_a=stage_a_gen(xsh[1]))
            segment(1, xsh[1], 0, 2048, ropes_d, ropes_d, 1024, ys[1024:2048, :], stage_a_done=True, next_stage_a=stage_a_gen(xph))
            segment(2, xph, 1, 8192, ropep_d, ropepq_d, 0, yp, stage_a_done=True)

        S.dry = True
        program()
        S.dry = False
        for k_ in state:
            state[k_] = 0
        for c in ctxr:
            c["tag"] = None
        program()
        for ch in S.out_chans:
            nc.sync.wait_ge(ch.sem, ch.count)
    return nc


_CACHE = {}


def _consts():
    if "c" in _CACHE:
        return _CACHE["c"]
    ident = np.eye(128, dtype=np.float32)
    perm = np.zeros((128, 128), np.float32)
    for m in range(128):
        partner = m + 32 if (m % 64) < 32 else m - 32
        perm[partner, m] = 1.0
    aind = np.zeros((2, 128), np.float32)
    aind[0, :64] = 1.0
    aind[1, 64:] = 1.0
    ropes = _rope_tables(2048)
    ropep = _rope_tables(8192)
    bm_s0 = _rowmask(32, 0).reshape(2, 96)
    bm_s1 = _rowmask(32, 16).reshape(2, 96)
    c = dict(ident=ident, perm=perm, aind=aind, ropes=ropes, ropep=ropep, bm_s0=bm_s0, bm_s1=bm_s1)
    _CACHE["c"] = c
    return c


def make_in_maps(x_prompt, x_sample, norm_g, w_in, na_q_g, na_k_g, na_rpb, gq_q_g, gq_k_g,
                 w_branch_a, w_branch_b, gate_bias, w_out, cores=range(NCORES)):
    c = _consts()
    f = lambda a: np.ascontiguousarray(np.asarray(a, dtype=np.float32))
    xp = f(x_prompt)[0]
    xsamp = f(x_sample)
    def tile_w(w, ncol):
        K_, N_ = w.shape
        return np.ascontiguousarray(w.reshape(K_ // 128, 128, N_ // ncol, ncol).transpose(2, 1, 0, 3).reshape(N_ // ncol, 128, (K_ // 128) * ncol))

    w_in0, w_a0, w_b0, w_o0 = tile_w(f(w_in)[0], 128), tile_w(f(w_branch_a)[0], 128), tile_w(f(w_branch_b)[0], 128), tile_w(f(w_out)[0], 256)
    gbc = np.ascontiguousarray(np.broadcast_to(f(norm_g)[0][None, :], (128, D)))
    hg = np.ascontiguousarray(np.stack([f(na_q_g)[0], f(na_k_g)[0], f(gq_q_g)[0], f(gq_k_g)[0]], axis=1))
    gbias = np.ascontiguousarray(f(gate_bias)[0].reshape(32, 128).T)
    rpb0 = f(na_rpb)[0]
    tbl = np.ascontiguousarray(np.stack([_bias_table(rpb0, 3, 10).reshape(128, 8, 1024),
                                         _bias_table(rpb0, 0, 14).reshape(128, 8, 1024)], axis=2))
    maps = []
    for core in cores:
        xs = xsamp[core]
        xsh = np.stack([_halo_rows(xs, 32, 0), _halo_rows(xs, 32, 16)], 0)
        xph = _halo_rows(xp, 128, 16 * core)
        bmask = np.concatenate([c["bm_s0"], c["bm_s1"], _rowmask(128, 16 * core).reshape(2, 96)], axis=1)
        ropepq = np.ascontiguousarray(c["ropep"][:, :, 1024 * core:1024 * (core + 1)])
        maps.append(dict(xs=xs, xsh=xsh, xp=xp, xph=xph, w_in=w_in0, w_a=w_a0, w_b=w_b0, w_o=w_o0,
                         gbc=gbc, hg=hg, gbias=gbias, tbl=tbl, ident=c["ident"], perm=c["perm"], aind=c["aind"],
                         bmask=np.ascontiguousarray(bmask), ropes=c["ropes"], ropep=c["ropep"], ropepq=ropepq))
    return maps


def kernel(x_prompt, x_sample, norm_g, w_in, na_q_g, na_k_g, na_rpb, gq_q_g, gq_k_g,
           w_branch_a, w_branch_b, gate_bias, w_out):
    maps = make_in_maps(x_prompt, x_sample, norm_g, w_in, na_q_g, na_k_g, na_rpb, gq_q_g, gq_k_g,
                        w_branch_a, w_branch_b, gate_bias, w_out)
    nc = build_program()
    res = run_bass_kernel_spmd(nc, maps, core_ids=list(range(NCORES)))
    y_prompt = np.concatenate([np.asarray(r["yp"], dtype=np.float32) for r in res.results], axis=0)[None]
    y_sample = np.stack([np.asarray(r["ys"], dtype=np.float32) for r in res.results], axis=0)
    return (y_prompt, y_sample)
```

```python
import numpy as np
from contextlib import ExitStack
import concourse.bass as bass
import concourse.mybir as mybir
from concourse.bass_utils import run_bass_kernel_spmd

F32 = mybir.dt.float32
BF16 = mybir.dt.bfloat16
AF = mybir.ActivationFunctionType
ALU = mybir.AluOpType

D = 2048
NCORES = 8
GRID_W = 64
HD = 128
SCALE = float(HD ** -0.5)
EPS = 1e-6
NEG = -1.0e5
TBL_NEG = -30000.0
SEG_T = 1024
HALO_T = 1536
C_NAQ, C_NAK, C_NAV, C_NAZ = 0, 1024, 2048, 3072
C_GQQ, C_GQK, C_GQV, C_GQZ = 4096, 5120, 5376, 5632
C_GA, C_GB = 6656, 8704
IN_W = 10752
NSLOT = 6
SAME_ENGINE_ALL_SYNC = False
EMBED_WAIT = True


class Chan:
    def __init__(self, sem, inc):
        self.sem, self.inc, self.count = sem, inc, 0


class Buf:
    __slots__ = ("name", "w", "r", "excl")

    def __init__(self, name, excl=False):
        self.name, self.w, self.r, self.excl = name, {}, {}, excl


class Eng:
    def __init__(self, name, handle, chan, is_pe=False):
        self.name, self.h, self.chan, self.is_pe = name, handle, chan, is_pe
        self.seen = {}


class Sched:
    def __init__(self, nc, es):
        self.nc, self.es = nc, es
        self.nsem = 0
        self.pe = Eng("pe", nc.tensor, self.chan(1), True)
        self.act = Eng("act", nc.scalar, self.chan(1))
        self.dve = Eng("dve", nc.vector, self.chan(1))
        self.pool = Eng("pool", nc.gpsimd, self.chan(1))
        self.sp = Eng("sp", nc.sync, None)
        self.out_chans = []
        self.dry = False

    def chan(self, inc=16):
        self.nsem += 1
        sem = self.es.enter_context(self.nc.semaphore("s%d" % self.nsem))
        return Chan(sem, inc)

    def issue(self, eng, fn, reads=(), writes=(), chan=None, inc=True):
        if self.dry:
            return None
        ch = chan if chan is not None else eng.chan
        deps = {}

        def add(ev, raw):
            c, v = ev
            if c is eng.chan:
                if eng.is_pe:
                    return
                if not (raw or SAME_ENGINE_ALL_SYNC):
                    return
            if deps.get(c, 0) < v:
                deps[c] = v

        for b in reads:
            for c, v in b.w.items():
                add((c, v), True)
            if b.excl:
                for c, v in b.r.items():
                    if c is not eng.chan:
                        add((c, v), False)
        for b in writes:
            for c, v in b.w.items():
                add((c, v), False)
            for c, v in b.r.items():
                add((c, v), False)
        need = [(c, v) for c, v in deps.items() if eng.seen.get(c, 0) < v]
        embed = need.pop() if (need and EMBED_WAIT and chan is None) else None
        for c, v in need:
            eng.h.wait_ge(c.sem, v)
            eng.seen[c] = v
        ins = fn()
        if embed is not None:
            ins._wait_ge(embed[0].sem, embed[1])
            eng.seen[embed[0]] = embed[1]
        if inc:
            ch.count += ch.inc
            ins.then_inc(ch.sem, ch.inc)
            val = ch.count
        else:
            val = ch.count + ch.inc
        for b in reads:
            if b.r.get(ch, 0) < val:
                b.r[ch] = val
        for b in writes:
            if b.w.get(ch, 0) < val:
                b.w[ch] = val
            b.r = {}
        return ins

    def fence(self, new_bufs, old_bufs):
        if self.dry:
            return
        ev = {}
        for b in old_bufs:
            for c, v in b.w.items():
                ev[c] = max(ev.get(c, 0), v)
            for c, v in b.r.items():
                ev[c] = max(ev.get(c, 0), v)
        for nb in new_bufs:
            for c, v in ev.items():
                if nb.r.get(c, 0) < v:
                    nb.r[c] = v


def _rope_tables(L):
    t = np.arange(L)
    r = (t // GRID_W).astype(np.float32)
    c = (t % GRID_W).astype(np.float32)
    freqs = (np.float32(10000.0) ** (-np.arange(0, 64, 2, dtype=np.float32) / np.float32(64))).astype(np.float32)
    ang_r = (r[:, None] * freqs[None, :]).astype(np.float32)
    ang_c = (c[:, None] * freqs[None, :]).astype(np.float32)
    cosT = np.empty((128, L), np.float32)
    sinT = np.empty((128, L), np.float32)
    for d in range(128):
        ang = ang_r if d < 64 else ang_c
        f = d % 32
        cosT[d] = np.cos(ang[:, f])
        s = np.sin(ang[:, f])
        sinT[d] = -s if (d % 64) < 32 else s
    return np.stack([cosT, sinT], 0)


def _slot_offs(i):
    if i == 0:
        return [3, 2, 1, 0, -1, -2]
    if i == 7:
        return [2, 1, 0, -1, -2, -3]
    return [2, 1, 0, -1, -2]


def _rowmask(R, r0):
    B = np.full((2, 8, 6, 2), NEG, np.float32)
    for i in range(8):
        for s, off in enumerate(_slot_offs(i)):
            lc = i + 2 + off
            j = r0 // 2 - 2 + lc
            for kr in range(2):
                krg = 2 * j + kr
                for qr in range(2):
                    r = r0 + 2 * i + qr
                    rs = min(max(r - 4, 0), R - 8)
                    if 0 <= krg < R and rs <= krg <= rs + 7:
                        B[kr, i, s, qr] = 0.0
    return B


def _bias_table(rpb, dr_lo=0, dr_hi=14):
    H = rpb.shape[0]
    tbl = np.full((128, H, 16, 64), TBL_NEG, np.float32)
    qc = np.arange(64)
    cs = np.clip(qc - 8, 0, 48)
    kc = np.arange(64)
    inwin = (kc[:, None] >= cs[None, :]) & (kc[:, None] < cs[None, :] + 16)
    dc = np.clip(kc[:, None] - qc[None, :] + 15, 0, 30)
    for kr in range(2):
        for e in range(16):
            dr = 15 - e + kr
            if dr_lo <= dr <= dr_hi:
                g = rpb[:, dr, :][:, dc]
                blk = np.where(inwin[None], g, np.float32(TBL_NEG))
                tbl[kr * 64:(kr + 1) * 64, :, e, :] = blk.transpose(1, 0, 2)
    return tbl


def _halo_rows(x_seq, R, r0):
    rows = np.clip(np.arange(r0 - 4, r0 + 20), 0, R - 1)
    idx = (rows[:, None] * GRID_W + np.arange(GRID_W)[None, :]).reshape(-1)
    return np.ascontiguousarray(x_seq[idx])


def build_program():
    nc = bass.Bass("TRN2", target_bir_lowering=False)

    def din(name, shape, dt=F32):
        return nc.dram_tensor(name, list(shape), dt, kind="ExternalInput").ap()

    xs = din("xs", [2048, D])
    xsh = din("xsh", [2, HALO_T, D])
    xp = din("xp", [8192, D])
    xph = din("xph", [HALO_T, D])
    w_in = din("w_in", [IN_W // 128, 128, 16 * 128])
    w_a = din("w_a", [16, 128, 8 * 128])
    w_b = din("w_b", [16, 128, 8 * 128])
    w_o = din("w_o", [8, 128, 16 * 256])
    gbc_d = din("gbc", [128, D])
    hg_d = din("hg", [128, 4])
    gb_d = din("gbias", [128, 32])
    tbl_d = din("tbl", [128, 8, 2, 1024])
    ident_d = din("ident", [128, 128])
    perm_d = din("perm", [128, 128])
    aind_d = din("aind", [2, 128])
    bmask_d = din("bmask", [2, 3 * 96])
    ropes_d = din("ropes", [2, 128, 2048])
    ropep_d = din("ropep", [2, 128, 8192])
    ropepq_d = din("ropepq", [2, 128, 1024])
    ys = nc.dram_tensor("ys", [2048, D], F32, kind="ExternalOutput").ap()
    yp = nc.dram_tensor("yp", [1024, D], F32, kind="ExternalOutput").ap()
    ksc = [[nc.dram_tensor("ksc%d%d" % (c, g), [128, L], BF16).ap() for g in range(2)]
           for c, L in enumerate((2048, 8192))]
    vsc = [[nc.dram_tensor("vsc%d%d" % (c, g), [128, L // 128, 130], BF16).ap() for g in range(2)]
           for c, L in enumerate((2048, 8192))]

    with ExitStack() as es:
        S = Sched(nc, es)
        PE, ACT, DVE, POOL, SP = S.pe, S.act, S.dve, S.pool, S.sp

        def sb(name, shape, dt):
            return es.enter_context(nc.sbuf_tensor("sb_" + name, list(shape), dt))

        gbc = sb("gbc", [128, D], F32)
        ident = sb("ident", [128, 128], BF16)
        perm = sb("perm", [128, 128], BF16)
        onesb = sb("onesb", [128, 128], BF16)
        hg = sb("hg", [128, 4], F32)
        gbias = sb("gbias", [128, 32], F32)
        aind = sb("aind", [2, 128], BF16)
        bmask = sb("bmask", [2, 3 * 96], BF16)
        epsb = sb("epsb", [128, 1], F32)
        tblbuf = [sb("tbl%d" % i, [128, 1040], F32) for i in range(2)]
        tbl = [t[:, 0:1024] for t in tblbuf]
        hT = sb("hT", [128, 16, HALO_T], BF16)
        oT = sb("oT", [128, 16, SEG_T], BF16)
        xt = [sb("xt%d" % i, [128, D], F32) for i in range(2)]
        xb = [sb("xb%d" % i, [128, D], BF16) for i in range(2)]
        wr = sb("wr", [128, NSLOT * 2048], BF16)
        arena = sb("arena", [128, 18560], BF16)
        pT = sb("pT", [128, 2, 768], BF16)
        tmpf = [sb("tmpf%d" % i, [128, 512], F32) for i in range(4)]
        qnb = [sb("qnb%d" % i, [128, 512], BF16) for i in range(2)]
        ropes = [sb("rope%d" % i, [128, 2, 512], F32) for i in range(2)]
        rc = sb("rc", [128, 8], F32)
        On = [sb("On%d" % i, [128, 4, 128], BF16) for i in range(2)]
        kst = [t[:, 0:512].bitcast(BF16).rearrange("p (g t) -> p g t", g=2) for t in tblbuf]
        vst = [t[:, 512:1032].bitcast(BF16).rearrange("p (g t c) -> p g t c", g=2, t=4) for t in tblbuf]
        yst = [tmpf[i // 2][:, (i % 2) * 256:(i % 2) * 256 + 256] for i in range(4)]
        xres = [tmpf[2 + i // 2][:, (i % 2) * 256:(i % 2) * 256 + 256] for i in range(4)]
        psA = es.enter_context(nc.psum_tensor("psA", [128, 6, 512], F32))
        psT = es.enter_context(nc.psum_tensor("psT", [128, 2, 8, 128], BF16))
        bA = [Buf("psA%d" % i, excl=True) for i in range(6)]
        bT = [Buf("psT%d" % i, excl=True) for i in range(2)]

        def av(off, shape):
            n = int(np.prod(shape))
            ap = arena[:, off:off + n]
            if len(shape) == 2:
                return ap.rearrange("p (a b) -> p a b", a=shape[0])
            return ap

        PH = 5144
        ph = []
        for i in range(2):
            o = i * PH
            ph.append(dict(qT=av(o, [1024]), kT=av(o + 1024, [1536]), vn=av(o + 2560, [12, 130]),
                           zT=av(o + 4120, [1024]),
                           b_q=Buf("qT%d" % i), b_k=Buf("kT%d" % i), b_v=Buf("vn%d" % i), b_z=Buf("zT%d" % i)))
        CR0 = 2 * PH
        ctxr = []
        for i in range(4):
            o = CR0 + i * 2064
            ctxr.append(dict(kT=av(o, [1024]), va=av(o + 1024, [8, 130]), buf=Buf("ctx%d" % i), chan=S.chan(),
                             tag=None))
        mT = arena[:, 0:16384].rearrange("p (a b) -> p a b", a=16)
        b_mT = Buf("mT")
        arena_att_bufs = [d[k] for d in ph for k in ("b_q", "b_k", "b_v", "b_z")] + [c["buf"] for c in ctxr]

        b_const = Buf("const")
        b_tbl = [Buf("tbl0"), Buf("tbl1")]
        b_hT = Buf("hT")
        b_oT = [Buf("oT0"), Buf("oT1")]
        b_xt = [Buf("xt0"), Buf("xt1")]
        b_xb = [Buf("xb0"), Buf("xb1")]
        b_pTa = [Buf("pTa0"), Buf("pTa1")]
        b_pTb = [Buf("pTb0"), Buf("pTb1")]
        b_pT = b_pTa + b_pTb
        b_pT3 = [Buf("pT3_%d" % i) for i in range(3)]
        b_tmp = [Buf("tmp%d" % i) for i in range(4)]
        b_qnb = [Buf("qnb0"), Buf("qnb1")]
        b_rope = [Buf("rope0"), Buf("rope1")]
        b_ss = [Buf("ss0"), Buf("ss1")]
        b_rstd = [Buf("rstd0"), Buf("rstd1")]
        b_rc = Buf("rc")
        b_On = [Buf("On0"), Buf("On1")]
        b_kst = b_tbl
        b_vst = b_tbl
        b_yst = [Buf("yst%d" % i) for i in range(4)]
        b_xres = [Buf("xres%d" % i) for i in range(4)]
        out_bufs_alias = b_yst + b_xres
        b_wr = [Buf("wr%d" % i) for i in range(NSLOT)]
        b_ksc = [[Buf("ksc%d%d" % (c, g)) for g in range(2)] for c in range(2)]

        ch_const = {"sp": S.chan(), "pool": S.chan()}
        ch_xt = [S.chan(), S.chan()]
        ch_tbl = [S.chan(), S.chan()]
        ch_rope = [S.chan(), S.chan()]
        ch_wr = [S.chan() for _ in range(NSLOT)]
        ch_kst = [S.chan(), S.chan()]
        ch_xres = [S.chan() for _ in range(4)]
        ch_yst = [S.chan() for _ in range(4)]
        S.out_chans = ch_yst

        pT_flat = pT[:, :, :].rearrange("p a b -> p (a b)")
        pT3 = [pT_flat[:, j * 512:(j + 1) * 512] for j in range(3)]
        S3 = [psA[:, 0, :], psA[:, 1, :], psT[:, 1, :, :].rearrange("p a b -> p (a b)").bitcast(F32)]
        bS3 = [bA[0], bA[1], bT[1]]
        hTc = [oT[:, 0:8, :].rearrange("p a b -> p (a b)").rearrange("p (a b) -> p a b", a=16),
               oT[:, 8:16, :].rearrange("p a b -> p (a b)").rearrange("p (a b) -> p a b", a=16)]

        def cload(q, dst, src):
            S.issue(q, lambda: q.h.dma_start(out=dst, in_=src), writes=[b_const], chan=ch_const[q.name])

        cload(SP, gbc[:], gbc_d)
        cload(SP, hg[:], hg_d)
        cload(SP, gbias[:], gb_d)
        cload(POOL, ident[:], ident_d)
        cload(POOL, perm[:], perm_d)
        cload(POOL, aind[:], aind_d)
        cload(POOL, bmask[:], bmask_d)
        S.issue(POOL, lambda: nc.gpsimd.memset(onesb[:], 1.0 / 128.0), writes=[b_const])
        S.issue(POOL, lambda: nc.gpsimd.memset(epsb[:], EPS), writes=[b_const])

        ring = {"free": list(range(NSLOT)), "plan": [], "pos": 0, "next": 0, "loaded": {}}
        LOOKAHEAD = 8

        def _try_load(idx):
            src_ap, kchunks, ncols = ring["plan"][idx]
            nsl = max(1, (kchunks * ncols) // 2048)
            start = None
            for s0 in sorted(ring["free"]):
                if nsl == 2 and s0 % 2:
                    continue
                if all((s0 + j) in ring["free"] for j in range(nsl)):
                    start = s0
                    break
            if start is None:
                return False
            slots = list(range(start, start + nsl))
            for s_ in slots:
                ring["free"].remove(s_)
            view = wr[:, start * 2048:start * 2048 + kchunks * ncols].rearrange("p (k c) -> p k c", k=kchunks)
            bufs = [b_wr[s_] for s_ in slots]
            if isinstance(src_ap, (list, tuple)):
                nb_ = len(src_ap)
                for bi, sa in enumerate(src_ap):
                    S.issue(POOL, lambda bi=bi, sa=sa: nc.gpsimd.dma_start(out=view[:, :, bi * 128:(bi + 1) * 128],
                                                                            in_=sa.rearrange("p (k c) -> p k c", k=kchunks)),
                            writes=bufs, chan=ch_wr[start])
            else:
                S.issue(POOL, lambda: nc.gpsimd.dma_start(out=view, in_=src_ap.rearrange("p (k c) -> p k c", k=kchunks)),
                        writes=bufs, chan=ch_wr[start])
            ring["loaded"][idx] = (view, bufs, slots)
            return True

        def _pump():
            while ring["next"] < len(ring["plan"]) and ring["next"] < ring["pos"] + LOOKAHEAD:
                if not _try_load(ring["next"]):
                    break
                ring["next"] += 1

        def wload(src_ap, kchunks, ncols):
            if S.dry:
                ring["plan"].append((src_ap, kchunks, ncols))
                view = wr[:, 0:kchunks * ncols].rearrange("p (k c) -> p k c", k=kchunks)
                return view, [], []
            idx = ring["pos"]
            ring["pos"] += 1
            while ring["next"] <= idx:
                ok = _try_load(ring["next"])
                assert ok, "weight ring exhausted"
                ring["next"] += 1
            res = ring["loaded"].pop(idx)
            _pump()
            return res

        def wfree(slots):
            if S.dry:
                return
            ring["free"].extend(slots)
            _pump()

        state = {"xi": 0, "ti": 0, "ri": 0, "pi": 0, "tbi": 0, "oni": 0, "ki": 0, "yi": 0}

        hT_flat = hT[:, :, :].rearrange("p a b -> p (a b)")
        NXB = 4
        stat = sb("stat", [128, 2 * (2 + NXB)], F32)
        xpool = []
        for i in range(2 + NXB):
            if i < 2:
                xt_ap, xb_ap = xt[i][:], xb[i][:]
            else:
                j = i - 2
                xt_ap = hT_flat[:, j * 4096:(j + 1) * 4096].bitcast(F32)
                xb_ap = hT_flat[:, 16384 + j * 2048:16384 + (j + 1) * 2048]
            xpool.append(dict(xt=xt_ap, xb=xb_ap, b_xt=Buf("xt%d" % i), b_xb=Buf("xb%d" % i), ch=S.chan(),
                              ss=stat[:, i:i + 1], rs=stat[:, 2 + NXB + i:3 + NXB + i], b_ss=Buf("ss%d" % i), b_rs=Buf("rs%d" % i)))
        xpool_extra_bufs = [p_[k_] for p_ in xpool[2:] for k_ in ("b_xt", "b_xb")]

        class TileStream:
            def __init__(self, tiles, pool):
                self.tiles, self.pool, self.nl, self.nc_ = tiles, pool, 0, 0

            def pump(self, upto):
                while self.nl < min(len(self.tiles), upto + 1):
                    k = self.nl
                    pb = self.pool[k % len(self.pool)]
                    x_ap = self.tiles[k][0]
                    S.issue(SP, lambda: nc.sync.dma_start(out=pb["xt"], in_=x_ap), writes=[pb["b_xt"]], chan=pb["ch"])
                    self.nl += 1

            def phase_a(self, k):
                self.pump(k + len(self.pool) - 1)
                pb = self.pool[k % len(self.pool)]
                S.issue(POOL, lambda: nc.gpsimd.memset(pb["ss"], 0.0), writes=[pb["b_ss"]])
                S.issue(ACT, lambda: nc.scalar.activation(out=pb["xb"], in_=pb["xt"], func=AF.Square,
                                                          scale=float(1.0 / np.sqrt(D)), accum_out=pb["ss"]),
                        reads=[pb["b_xt"], pb["b_ss"]], writes=[pb["b_xb"], pb["b_ss"]])
                S.issue(ACT, lambda: nc.scalar.activation(out=pb["rs"], in_=pb["ss"], func=AF.Ln, bias=epsb[:]),
                        reads=[pb["b_ss"], b_const], writes=[pb["b_rs"]])
                S.issue(ACT, lambda: nc.scalar.activation(out=pb["rs"], in_=pb["rs"], func=AF.Exp, scale=-0.5),
                        reads=[pb["b_rs"]], writes=[pb["b_rs"]])
                S.issue(DVE, lambda: nc.vector.scalar_tensor_tensor(out=pb["xb"], in0=pb["xt"], scalar=pb["rs"],
                                                                    in1=gbc[:], op0=ALU.mult, op1=ALU.mult),
                        reads=[pb["b_xt"], pb["b_rs"], b_const], writes=[pb["b_xb"]])

            def phase_b(self, k):
                pb = self.pool[k % len(self.pool)]
                _, dst, dst_buf = self.tiles[k]
                for hf in range(2):
                    for kk in range(8):
                        c0 = (hf * 8 + kk) * 128
                        S.issue(PE, lambda: nc.tensor.transpose(psT[:, hf, kk, :], pb["xb"][:, c0:c0 + 128], ident[:]),
                                reads=[pb["b_xb"], b_const], writes=[bT[hf]], inc=(kk == 7))
                    if hf == 0:
                        S.issue(ACT, lambda: nc.scalar.copy(out=dst[:, 0:8, :], in_=psT[:, 0, :, :]), reads=[bT[0]], writes=[dst_buf])
                    else:
                        S.issue(DVE, lambda: nc.vector.tensor_copy(out=dst[:, 8:16, :], in_=psT[:, 1, :, :]), reads=[bT[1]], writes=[dst_buf])

            def compute_next(self):
                k = self.nc_
                self.nc_ += 1
                if k == 0:
                    self.phase_a(0)
                if k + 1 < len(self.tiles):
                    self.phase_a(k + 1)
                self.phase_b(k)

            def finish(self):
                while self.nc_ < len(self.tiles):
                    self.compute_next()

        def proj_fm(wv, wbufs, src, src_buf, t0, nt, bank):
            for k in range(16):
                S.issue(PE, lambda k=k: nc.tensor.matmul(psA[:, bank, 0:nt], lhsT=wv[:, k, :], rhs=src[:, k, t0:t0 + nt],
                                                         start=(k == 0), stop=(k == 15)),
                        reads=wbufs + [src_buf], writes=[bA[bank]], inc=(k == 15))

        def rope_load(tab_d, p0, nt):
            j = state["ri"]
            state["ri"] ^= 1
            S.issue(SP, lambda: nc.sync.dma_start(out=ropes[j][:, :, 0:nt], in_=tab_d[:, :, p0:p0 + nt].rearrange("a p t -> p a t")),
                    writes=[b_rope[j]], chan=ch_rope[j])
            return j

        SETS = [dict(q=0, ms=2, pm=4, tA=tmpf[0], tC=tmpf[1], yb=qnb[0], bA_=b_tmp[0], bC_=b_tmp[1], bY=b_qnb[0]),
                dict(q=1, ms=3, pm=5, tA=tmpf[2], tC=tmpf[3], yb=qnb[1], bA_=b_tmp[2], bC_=b_tmp[3], bY=b_qnb[1])]

        def blk_s1(B, st):
            kind = B["kind"]
            q = st["q"]
            if kind in ("norm", "silu", "vT"):
                proj_fm(B["w"][0], B["w"][1], B["src"], B["src_buf"], B["t0"], 512, q)
            if kind == "vT":
                S.issue(ACT, lambda: nc.scalar.copy(out=st["yb"][:], in_=psA[:, q, :]), reads=[bA[q]], writes=[st["bY"]])
            if kind == "norm":
                g = B["g"]
                rope_j = B.get("rope")
                S.issue(ACT, lambda: nc.scalar.activation(out=st["tA"][:, 0:256].bitcast(BF16), in_=psA[:, q, :], func=AF.Square),
                        reads=[bA[q]], writes=[st["bA_"]])
                if rope_j is not None:
                    S.issue(ACT, lambda: nc.scalar.activation(out=st["yb"][:], in_=psA[:, q, :], func=AF.Copy, scale=hg[:, g:g + 1]),
                            reads=[bA[q], b_const], writes=[st["bY"]])
                    S.issue(DVE, lambda: nc.vector.scalar_tensor_tensor(out=st["tC"][:], in0=psA[:, q, :], scalar=hg[:, g:g + 1],
                                                                        in1=ropes[rope_j][:, 0, :], op0=ALU.mult, op1=ALU.mult),
                            reads=[bA[q], b_rope[rope_j], b_const], writes=[st["bC_"]])
                else:
                    S.issue(DVE, lambda: nc.vector.tensor_scalar(out=st["tC"][:], in0=psA[:, q, :], scalar1=hg[:, g:g + 1], scalar2=None,
                                                                 op0=ALU.mult),
                            reads=[bA[q], b_const], writes=[st["bC_"]])
            elif kind == "silu":
                return lambda: S.issue(ACT, lambda: nc.scalar.activation(out=B["out"], in_=psA[:, q, :], func=AF.Silu),
                                       reads=[bA[q]], writes=[B["out_buf"]])
            elif kind == "vtm":
                nt_, ncol = B["ntiles"], B["ncols"]
                for t in range(nt_):
                    tk = B["t0"] + t * 128
                    for k in range(16):
                        S.issue(PE, lambda k=k, t=t, tk=tk: nc.tensor.matmul(psA[:, q, t * ncol:(t + 1) * ncol], lhsT=B["src"][:, k, tk:tk + 128],
                                                                            rhs=B["w"][0][:, k, :], start=(k == 0), stop=(k == 15)),
                                reads=B["w"][1] + [B["src_buf"]], writes=[bA[q]], inc=(k == 15))
                B["evac"](q)

        def blk_s2(B, st):
            if B["kind"] == "vT":
                tb = st["q"] % 2
                for t in range(4):
                    S.issue(PE, lambda t=t: nc.tensor.transpose(psT[:, tb, t, :], st["yb"][:, t * 128:(t + 1) * 128], ident[:]),
                            reads=[st["bY"], b_const], writes=[bT[tb]], inc=(t == 3))
                S.issue(DVE, lambda: nc.vector.tensor_copy(out=B["out"], in_=psT[:, tb, 0:4, :]), reads=[bT[tb]], writes=[B["out_buf"]])
                return
            if B["kind"] != "norm":
                return
            ms, pm = st["ms"], st["pm"]
            rope_j = B.get("rope")
            S.issue(PE, lambda: nc.tensor.matmul(psA[:, ms, :], lhsT=onesb[:], rhs=st["tA"][:, 0:256].bitcast(BF16), start=True, stop=True),
                    reads=[st["bA_"], b_const], writes=[bA[ms]])
            if rope_j is not None:
                S.issue(PE, lambda: nc.tensor.matmul(psA[:, pm, :], lhsT=perm[:], rhs=st["yb"][:], start=True, stop=True),
                        reads=[st["bY"], b_const], writes=[bA[pm]])
            S.issue(ACT, lambda: nc.scalar.activation(out=psA[:, ms, :], in_=psA[:, ms, :], func=AF.Ln, bias=epsb[:]),
                    reads=[bA[ms], b_const], writes=[bA[ms]])
            S.issue(ACT, lambda: nc.scalar.activation(out=psA[:, ms, :], in_=psA[:, ms, :], func=AF.Exp, scale=-0.5),
                    reads=[bA[ms]], writes=[bA[ms]])
            if rope_j is not None:
                S.issue(DVE, lambda: nc.vector.tensor_tensor(out=st["tA"][:], in0=psA[:, pm, :], in1=ropes[rope_j][:, 1, :], op=ALU.mult),
                        reads=[bA[pm], b_rope[rope_j]], writes=[st["bA_"]])
                S.issue(DVE, lambda: nc.vector.tensor_tensor(out=st["tC"][:], in0=st["tC"][:], in1=st["tA"][:], op=ALU.add),
                        reads=[st["bC_"], st["bA_"]], writes=[st["bC_"]])
            S.issue(DVE, lambda: nc.vector.tensor_tensor(out=B["out"], in0=st["tC"][:], in1=psA[:, ms, :], op=ALU.mult),
                    reads=[st["bC_"], bA[ms]], writes=[B["out_buf"]])

        def run_blocks_gen(blocks):
            for idx, B in enumerate(blocks):
                deferred = blk_s1(B, SETS[idx % 2])
                if idx >= 1:
                    blk_s2(blocks[idx - 1], SETS[(idx - 1) % 2])
                if deferred is not None:
                    deferred()
                yield
            if blocks:
                blk_s2(blocks[-1], SETS[(len(blocks) - 1) % 2])

        def run_blocks(blocks):
            for _ in run_blocks_gen(blocks):
                pass

        def ctx_prepass(ci, x_ctx, C, rope_tab):
            wk = [wload(w_in[C_GQK // 128 + g], 16, 128) for g in range(2)]
            wvv = [wload(w_in[C_GQV // 128 + g], 16, 128) for g in range(2)]
            for i in range(2):
                S.issue(POOL, lambda i=i: nc.gpsimd.memset(vst[i], 1.0), writes=[b_vst[i]])
            nb = C // 512
            S.fence(xpool_extra_bufs, [b_hT])
            tiles = []
            for blk in range(nb):
                hb = blk % 2
                for t in range(4):
                    tok = blk * 512 + t * 128
                    tiles.append((x_ctx[tok:tok + 128, :], hTc[hb][:, :, t * 128:(t + 1) * 128], b_oT[hb]))
            ts = TileStream(tiles, xpool)
            pending_store = []

            def flush_store(upto=None):
                while pending_store and (upto is None or pending_store[0][0] <= upto):
                    blk_, si_ = pending_store.pop(0)
                    for g in range(2):
                        S.issue(SP, lambda g=g: nc.sync.dma_start(out=ksc[ci][g][:, blk_ * 512:(blk_ + 1) * 512], in_=kst[si_][:, g, :]),
                                reads=[b_kst[si_]], writes=[b_ksc[ci][g]], chan=ch_kst[si_])
                        S.issue(SP, lambda g=g: nc.sync.dma_start(out=vsc[ci][g][:, blk_ * 4:(blk_ + 1) * 4, :], in_=vst[si_][:, g, :, :]),
                                reads=[b_vst[si_]], writes=[b_ksc[ci][g]], chan=ch_kst[si_])

            for t in range(4):
                ts.compute_next()
            for blk in range(nb):
                hb = blk % 2
                flush_store(blk - 2)
                j = rope_load(rope_tab, blk * 512, 512)
                si = state["ki"]
                state["ki"] ^= 1
                blocks = []
                for g in range(2):
                    blocks.append(dict(kind="norm", w=wk[g], src=hTc[hb], src_buf=b_oT[hb], t0=0, g=3, rope=j,
                                       out=kst[si][:, g, :], out_buf=b_kst[si]))
                for g in range(2):
                    blocks.append(dict(kind="vT", w=wvv[g], src=hTc[hb], src_buf=b_oT[hb], t0=0,
                                       out=vst[si][:, g, :, 0:128], out_buf=b_vst[si]))
                gen = run_blocks_gen(blocks)
                for t in range(4):
                    if blk + 1 < nb:
                        ts.compute_next()
                    next(gen, None)
                for _ in gen:
                    pass
                pending_store.append((blk, si))
            flush_store()
            S.fence([b_hT], xpool_extra_bufs)
            for w_ in wk + wvv:
                wfree(w_[2])

        def ctx_block(ci, g, blk):
            tag = (ci, g, blk)
            for c in ctxr:
                if c["tag"] == tag:
                    return c
            c = ctxr[state["ti"] % 4]
            state["ti"] += 1
            c["tag"] = tag
            S.issue(SP, lambda: nc.sync.dma_start(out=c["kT"], in_=ksc[ci][g][:, blk * 1024:(blk + 1) * 1024]),
                    reads=[b_ksc[ci][g]], writes=[c["buf"]], chan=c["chan"])
            S.issue(SP, lambda: nc.sync.dma_start(out=c["va"], in_=vsc[ci][g][:, blk * 8:(blk + 1) * 8, :]),
                    reads=[b_ksc[ci][g]], writes=[c["buf"]], chan=c["chan"])
            return c

        def finish_o(rc_src, o_src, banks, hidx, q0, P, tb=None):
            oi = state["oni"]
            state["oni"] ^= 1
            if tb is None:
                tb = oi
            n = len(banks)
            bset = sorted(set(bk for bk, _ in banks))
            S.issue(DVE, lambda: nc.vector.reciprocal(out=rc[:, 0:n], in_=rc_src), reads=[bA[b_] for b_ in bset], writes=[b_rc])
            S.issue(DVE, lambda: nc.vector.tensor_tensor(out=On[oi][:, 0:n, :], in0=o_src, in1=rc[:, 0:n].unsqueeze(2).to_broadcast([128, n, 128]),
                                                         op=ALU.mult),
                    reads=[bA[b_] for b_ in bset] + [b_rc], writes=[b_On[oi]])
            for t in range(n):
                S.issue(PE, lambda t=t: nc.tensor.transpose(psT[:, tb, t, :], On[oi][:, t, :], ident[:]),
                        reads=[b_On[oi], b_const], writes=[bT[tb]], inc=(t == n - 1))
            S.issue(DVE, lambda: nc.vector.tensor_tensor(out=oT[:, hidx, q0:q0 + n * 128].rearrange("p (a b) -> p a b", a=n),
                                                         in0=psT[:, tb, 0:n, :],
                                                         in1=P["zT"][:, q0:q0 + n * 128].rearrange("p (a b) -> p a b", a=n), op=ALU.mult),
                    reads=[bT[tb], P["b_z"]], writes=[b_oT[hidx // 8]])

        def stage_a_gen(xh):
            ts = TileStream([(xh[t * 128:(t + 1) * 128, :], hT[:, :, t * 128:(t + 1) * 128], b_hT) for t in range(12)], xpool[0:2])
            ts.pump(1)
            yield
            for _ in range(12):
                ts.compute_next()
                yield

        def segment(si, xh, ci, C, rope_ctx, rope_q, q_off, y_out, stage_a_done=False, next_stage_a=None):
            S.fence(arena_att_bufs, [b_mT])
            for c in ctxr:
                c["tag"] = None
            if not stage_a_done:
                for _ in stage_a_gen(xh):
                    pass
            for P in ph:
                S.issue(POOL, lambda P=P: nc.gpsimd.memset(P["vn"][:, :, 128:130], 1.0), writes=[P["b_v"]])
            S.fence(b_pT3, b_pT)
            hcount = 0
            for g in range(2):
                for hq in range(4):
                    head = 4 * g + hq
                    P = ph[hcount % 2]
                    hcount += 1
                    wq = wload(w_in[C_GQQ // 128 + head], 16, 128)
                    wz = wload(w_in[C_GQZ // 128 + head], 16, 128)
                    blocks = []
                    for qb in range(2):
                        j = rope_load(rope_q, q_off + qb * 512, 512)
                        blocks.append(dict(kind="norm", w=wq, src=hT, src_buf=b_hT, t0=256 + qb * 512, g=2, rope=j,
                                           out=P["qT"][:, qb * 512:(qb + 1) * 512], out_buf=P["b_q"]))
                    for qb in range(2):
                        blocks.append(dict(kind="silu", w=wz, src=hT, src_buf=b_hT, t0=256 + qb * 512,
                                           out=P["zT"][:, qb * 512:(qb + 1) * 512], out_buf=P["b_z"]))
                    run_blocks(blocks)
                    wfree(wq[2]); wfree(wz[2])
                    nblk = C // 1024
                    chunks = [(qb, blk, kc) for qb in range(2) for blk in range(nblk) for kc in range(8)]
                    n = len(chunks)
                    nper = nblk * 8
                    cblk = {}

                    def emit_S(idx):
                        qb, blk, kc = chunks[idx]
                        c = ctx_block(ci, g, blk)
                        if kc == 0 and blk + 1 < nblk:
                            ctx_block(ci, g, blk + 1)
                        cblk[idx] = c
                        j3 = idx % 3
                        S.issue(PE, lambda: nc.tensor.matmul(S3[j3], lhsT=c["kT"][:, kc * 128:(kc + 1) * 128],
                                                             rhs=P["qT"][:, qb * 512:(qb + 1) * 512], start=True, stop=True),
                                reads=[c["buf"], P["b_q"]], writes=[bS3[j3]])
                        S.issue(ACT, lambda: nc.scalar.activation(out=pT3[j3], in_=S3[j3], func=AF.Exp, scale=SCALE),
                                reads=[bS3[j3]], writes=[b_pT3[j3]])

                    def emit_PV(idx):
                        qb, blk, kc = chunks[idx]
                        c = cblk.pop(idx)
                        j3 = idx % 3
                        first, last = (idx % nper == 0), (idx % nper == nper - 1)
                        for t in range(4):
                            S.issue(PE, lambda t=t: nc.tensor.matmul(psA[:, 2 + t, 0:129], lhsT=pT3[j3][:, t * 128:(t + 1) * 128],
                                                                     rhs=c["va"][:, kc, 0:129], start=first, stop=last),
                                    reads=[b_pT3[j3], c["buf"]], writes=[bA[2 + t]], inc=(last or t == 3))
                        if last:
                            finish_o(psA[:, 2:6, 128], psA[:, 2:6, 0:128], [(2 + t, 0) for t in range(4)], 8 + head, qb * 512, P, tb=0)

                    emit_S(0)
                    emit_S(1)
                    for idx in range(n):
                        if idx + 2 < n:
                            emit_S(idx + 2)
                        emit_PV(idx)
            S.fence(b_pT, b_pT3)
            for h in range(8):
                P = ph[h % 2]
                for v_ in range(2):
                    S.issue(SP, lambda v_=v_: nc.sync.dma_start(out=tbl[v_], in_=tbl_d[:, h, v_, :]), writes=[b_tbl[v_]], chan=ch_tbl[v_])
                edge_q = {0: (0, 1), 1: (6, 7), 2: (0, 1, 6, 7)}[si]
                wq = wload(w_in[C_NAQ // 128 + h], 16, 128)
                wk = wload(w_in[C_NAK // 128 + h], 16, 128)
                wv = wload(w_in[C_NAV // 128 + h], 16, 128)
                wz = wload(w_in[C_NAZ // 128 + h], 16, 128)
                blocks = []
                for qb in range(2):
                    blocks.append(dict(kind="norm", w=wq, src=hT, src_buf=b_hT, t0=256 + qb * 512, g=0,
                                       out=P["qT"][:, qb * 512:(qb + 1) * 512], out_buf=P["b_q"]))
                for kb in range(3):
                    blocks.append(dict(kind="norm", w=wk, src=hT, src_buf=b_hT, t0=kb * 512, g=1,
                                       out=P["kT"][:, kb * 512:(kb + 1) * 512], out_buf=P["b_k"]))
                for vb in range(3):
                    blocks.append(dict(kind="vT", w=wv, src=hT, src_buf=b_hT, t0=vb * 512,
                                       out=P["vn"][:, vb * 4:(vb + 1) * 4, 0:128], out_buf=P["b_v"]))
                for qb in range(2):
                    blocks.append(dict(kind="silu", w=wz, src=hT, src_buf=b_hT, t0=256 + qb * 512,
                                       out=P["zT"][:, qb * 512:(qb + 1) * 512], out_buf=P["b_z"]))
                run_blocks(blocks)
                for w_ in (wq, wk, wv, wz):
                    wfree(w_[2])
                def na_S(i):
                    offs = _slot_offs(i)
                    nsl = len(offs)
                    rg = (i % 2) * 2
                    pb = i % 2
                    edge = i in edge_q
                    ti = 1 if edge else 0
                    e0 = 8 - 2 * offs[0]
                    for half, (s0, s1, bk, bpt) in enumerate(((0, 3, rg, b_pTa[pb]), (3, nsl, rg + 1, b_pTb[pb]))):
                        for s_ in range(s0, s1):
                            lc = i + 2 + offs[s_]
                            c0 = (s_ - s0) * 128
                            grp_end = (s_ == s1 - 1)
                            S.issue(PE, lambda: nc.tensor.matmul(psA[:, bk, c0:c0 + 128], lhsT=P["kT"][:, lc * 128:(lc + 1) * 128],
                                                                 rhs=P["qT"][:, i * 128:(i + 1) * 128], start=True, stop=(not edge)),
                                    reads=[P["b_k"], P["b_q"]], writes=[bA[bk]], inc=(grp_end and not edge))
                            if edge:
                                mi = si * 96 + (i * 6 + s_) * 2
                                S.issue(PE, lambda: nc.tensor.matmul(psA[:, bk, c0:c0 + 128], lhsT=aind[:, :],
                                                                     rhs=bmask[:, mi:mi + 2].unsqueeze(2).to_broadcast([2, 2, 64]),
                                                                     start=False, stop=True),
                                        reads=[b_const], writes=[bA[bk]], inc=grp_end)
                        w_ = (s1 - s0) * 128
                        S.issue(DVE, lambda: nc.vector.scalar_tensor_tensor(out=psA[:, bk, 0:w_], in0=psA[:, bk, 0:w_], scalar=SCALE,
                                                                            in1=tbl[ti][:, (e0 + 2 * s0) * 64:(e0 + 2 * s1) * 64],
                                                                            op0=ALU.mult, op1=ALU.add),
                                reads=[bA[bk], b_tbl[ti]], writes=[bA[bk]])
                        S.issue(ACT, lambda: nc.scalar.activation(out=pT[:, pb, s0 * 128:s1 * 128], in_=psA[:, bk, 0:w_], func=AF.Exp),
                                reads=[bA[bk]], writes=[bpt])

                def na_PV(i):
                    offs = _slot_offs(i)
                    nsl = len(offs)
                    pb = i % 2
                    ob = 4 + (i // 2) % 2
                    oc = (i % 2) * 256
                    for s_, off in enumerate(offs):
                        lc = i + 2 + off
                        bpt = b_pTa[pb] if s_ < 3 else b_pTb[pb]
                        S.issue(PE, lambda: nc.tensor.matmul(psA[:, ob, oc:oc + 129], lhsT=pT[:, pb, s_ * 128:(s_ + 1) * 128],
                                                             rhs=P["vn"][:, lc, 0:129], start=(s_ == 0), stop=(s_ == nsl - 1)),
                                reads=[bpt, P["b_v"]], writes=[bA[ob]], inc=(s_ == nsl - 1 or s_ == 2))
                    if i % 2 == 1:
                        finish_o(psA[:, ob, :].rearrange("p (a b) -> p a b", a=2)[:, :, 128],
                                 psA[:, ob, :].rearrange("p (a b) -> p a b", a=2)[:, :, 0:128], [(ob, 0), (ob, 256)], h, (i - 1) * 128, P)

                na_S(0)
                for i in range(8):
                    if i + 1 < 8:
                        na_S(i + 1)
                    na_PV(i)
            S.fence([b_mT], arena_att_bufs)
            if next_stage_a is not None:
                next(next_stage_a, None)
            for cb in range(16):
                wga = wload(w_in[C_GA // 128 + cb], 16, 128)
                wgb = wload(w_in[C_GB // 128 + cb], 16, 128)
                wa = wload(w_a[cb], 8, 128)
                wb = wload(w_b[cb], 8, 128)
                for tb in range(2):
                    t0 = tb * 512
                    proj_fm(wga[0], wga[1], hT, b_hT, 256 + t0, 512, 0)
                    proj_fm(wgb[0], wgb[1], hT, b_hT, 256 + t0, 512, 1)
                    for br, (wx, bank) in enumerate(((wa, 2), (wb, 3))):
                        for k in range(8):
                            S.issue(PE, lambda k=k, wx=wx, bank=bank, br=br: nc.tensor.matmul(psA[:, bank, :], lhsT=wx[0][:, k, :],
                                                                                             rhs=oT[:, br * 8 + k, t0:t0 + 512], start=(k == 0), stop=(k == 7)),
                                    reads=wx[1] + [b_oT[br]], writes=[bA[bank]], inc=(k == 7))
                    S.issue(ACT, lambda: nc.scalar.activation(out=tmpf[0][:], in_=psA[:, 0, :], func=AF.Sigmoid, bias=gbias[:, cb:cb + 1]),
                            reads=[bA[0], b_const], writes=[b_tmp[0]])
                    S.issue(ACT, lambda: nc.scalar.activation(out=tmpf[1][:], in_=psA[:, 1, :], func=AF.Sigmoid, bias=gbias[:, 16 + cb:17 + cb]),
                            reads=[bA[1], b_const], writes=[b_tmp[1]])
                    S.issue(DVE, lambda: nc.vector.tensor_tensor(out=tmpf[2][:], in0=psA[:, 2, :], in1=tmpf[0][:], op=ALU.mult),
                            reads=[bA[2], b_tmp[0]], writes=[b_tmp[2]])
                    S.issue(DVE, lambda: nc.vector.tensor_tensor(out=tmpf[3][:], in0=psA[:, 3, :], in1=tmpf[1][:], op=ALU.mult),
                            reads=[bA[3], b_tmp[1]], writes=[b_tmp[3]])
                    S.issue(POOL, lambda: nc.gpsimd.tensor_tensor(out=mT[:, cb, t0:t0 + 512], in0=tmpf[2][:], in1=tmpf[3][:], op=ALU.add),
                            reads=[b_tmp[2], b_tmp[3]], writes=[b_mT])
                for w_ in (wga, wgb, wa, wb):
                    wfree(w_[2])
            S.fence(out_bufs_alias, b_tmp)
            oblks = [(ob, tt) for ob in range(8) for tt in range(8)]

            def xres_load(n):
                ob, tt = oblks[n]
                S.issue(SP, lambda: nc.sync.dma_start(out=xres[n % 4], in_=xh[256 + tt * 128:256 + (tt + 1) * 128, ob * 256:(ob + 1) * 256]),
                        writes=[b_xres[n % 4]], chan=ch_xres[n % 4])

            for n in range(3):
                xres_load(n)
            wo = None
            for n, (ob, tt) in enumerate(oblks):
                if tt == 0:
                    wo = wload(w_o[ob], 16, 256)
                if n + 3 < len(oblks):
                    xres_load(n + 3)
                yi = n % 4
                bank = 4 + (n % 2)
                for k in range(16):
                    S.issue(PE, lambda k=k: nc.tensor.matmul(psA[:, bank, 0:256], lhsT=mT[:, k, tt * 128:(tt + 1) * 128], rhs=wo[0][:, k, :],
                                                             start=(k == 0), stop=(k == 15)),
                            reads=wo[1] + [b_mT], writes=[bA[bank]], inc=(k == 15))
                S.issue(DVE, lambda: nc.vector.tensor_tensor(out=yst[yi], in0=psA[:, bank, 0:256], in1=xres[yi], op=ALU.add),
                        reads=[bA[bank], b_xres[yi]], writes=[b_yst[yi]])
                S.issue(SP, lambda: nc.sync.dma_start(out=y_out[tt * 128:(tt + 1) * 128, ob * 256:(ob + 1) * 256], in_=yst[yi]),
                        reads=[b_yst[yi]], chan=ch_yst[yi])
                if next_stage_a is not None and n % 5 == 2:
                    next(next_stage_a, None)
                if tt == 7:
                    wfree(wo[2])
            S.fence(b_tmp, out_bufs_alias)
            if next_stage_a is not None:
                for _ in next_stage_a:
                    pass

        def program():
            ctx_prepass(0, xs, 2048, ropes_d)
            ctx_prepass(1, xp, 8192, ropep_d)
            segment(0, xsh[0], 0, 2048, ropes_d, ropes_d, 0, ys[0:1024, :], next_stage_a=stage_a_gen(xsh[1]))
            segment(1, xsh[1], 0, 2048, ropes_d, ropes_d, 1024, ys[1024:2048, :], stage_a_done=True, next_stage_a=stage_a_gen(xph))
            segment(2, xph, 1, 8192, ropep_d, ropepq_d, 0, yp, stage_a_done=True)

        S.dry = True
        program()
        S.dry = False
        for k_ in state:
            state[k_] = 0
        for c in ctxr:
            c["tag"] = None
        program()
        for ch in S.out_chans:
            nc.sync.wait_ge(ch.sem, ch.count)
    return nc


_CACHE = {}


def _consts():
    if "c" in _CACHE:
        return _CACHE["c"]
    ident = np.eye(128, dtype=np.float32)
    perm = np.zeros((128, 128), np.float32)
    for m in range(128):
        partner = m + 32 if (m % 64) < 32 else m - 32
        perm[partner, m] = 1.0
    aind = np.zeros((2, 128), np.float32)
    aind[0, :64] = 1.0
    aind[1, 64:] = 1.0
    ropes = _rope_tables(2048)
    ropep = _rope_tables(8192)
    bm_s0 = _rowmask(32, 0).reshape(2, 96)
    bm_s1 = _rowmask(32, 16).reshape(2, 96)
    c = dict(ident=ident, perm=perm, aind=aind, ropes=ropes, ropep=ropep, bm_s0=bm_s0, bm_s1=bm_s1)
    _CACHE["c"] = c
    return c


def make_in_maps(x_prompt, x_sample, norm_g, w_in, na_q_g, na_k_g, na_rpb, gq_q_g, gq_k_g,
                 w_branch_a, w_branch_b, gate_bias, w_out, cores=range(NCORES)):
    c = _consts()
    f = lambda a: np.ascontiguousarray(np.asarray(a, dtype=np.float32))
    xp = f(x_prompt)[0]
    xsamp = f(x_sample)
    def tile_w(w, ncol):
        K_, N_ = w.shape
        return np.ascontiguousarray(w.reshape(K_ // 128, 128, N_ // ncol, ncol).transpose(2, 1, 0, 3).reshape(N_ // ncol, 128, (K_ // 128) * ncol))

    w_in0, w_a0, w_b0, w_o0 = tile_w(f(w_in)[0], 128), tile_w(f(w_branch_a)[0], 128), tile_w(f(w_branch_b)[0], 128), tile_w(f(w_out)[0], 256)
    gbc = np.ascontiguousarray(np.broadcast_to(f(norm_g)[0][None, :], (128, D)))
    hg = np.ascontiguousarray(np.stack([f(na_q_g)[0], f(na_k_g)[0], f(gq_q_g)[0], f(gq_k_g)[0]], axis=1))
    gbias = np.ascontiguousarray(f(gate_bias)[0].reshape(32, 128).T)
    rpb0 = f(na_rpb)[0]
    tbl = np.ascontiguousarray(np.stack([_bias_table(rpb0, 3, 10).reshape(128, 8, 1024),
                                         _bias_table(rpb0, 0, 14).reshape(128, 8, 1024)], axis=2))
    maps = []
    for core in cores:
        xs = xsamp[core]
        xsh = np.stack([_halo_rows(xs, 32, 0), _halo_rows(xs, 32, 16)], 0)
        xph = _halo_rows(xp, 128, 16 * core)
        bmask = np.concatenate([c["bm_s0"], c["bm_s1"], _rowmask(128, 16 * core).reshape(2, 96)], axis=1)
        ropepq = np.ascontiguousarray(c["ropep"][:, :, 1024 * core:1024 * (core + 1)])
        maps.append(dict(xs=xs, xsh=xsh, xp=xp, xph=xph, w_in=w_in0, w_a=w_a0, w_b=w_b0, w_o=w_o0,
                         gbc=gbc, hg=hg, gbias=gbias, tbl=tbl, ident=c["ident"], perm=c["perm"], aind=c["aind"],
                         bmask=np.ascontiguousarray(bmask), ropes=c["ropes"], ropep=c["ropep"], ropepq=ropepq))
    return maps


def kernel(x_prompt, x_sample, norm_g, w_in, na_q_g, na_k_g, na_rpb, gq_q_g, gq_k_g,
           w_branch_a, w_branch_b, gate_bias, w_out):
    maps = make_in_maps(x_prompt, x_sample, norm_g, w_in, na_q_g, na_k_g, na_rpb, gq_q_g, gq_k_g,
                        w_branch_a, w_branch_b, gate_bias, w_out)
    nc = build_program()
    res = run_bass_kernel_spmd(nc, maps, core_ids=list(range(NCORES)))
    y_prompt = np.concatenate([np.asarray(r["yp"], dtype=np.float32) for r in res.results], axis=0)[None]
    y_sample = np.stack([np.asarray(r["ys"], dtype=np.float32) for r in res.results], axis=0)
    return (y_prompt, y_sample)
```
